# Optimizing a Trainium2 kernel written in Bass

```python
import math
import jax
import jax.numpy as jnp
from jax import lax
import numpy as np

D_MODEL = 2048
BATCH = 2
SEQ = 8192
DEPTH = 4

GRID_W = 64
CTX_LEN = 256
N_MIXERS = 3
N_A = (DEPTH + 2) // 3
N_B = (DEPTH + 1) // 3
N_C = DEPTH // 3
D_FF = 4 * D_MODEL
N_MOD = 6
EPS = 1e-6
ROPE_BASE = 10000.0
Q_BLOCK = 128

CHUNK = 128
GMLP_WIDTH = D_MODEL
GMLP_GROUPS = 16
GMLP_GROUP_DIM = GMLP_WIDTH // GMLP_GROUPS

DIFF_HEADS = D_MODEL // 256
DIFF_HEAD_DIM = 128
DIFF_V_DIM = 2 * DIFF_HEAD_DIM

MLA_HEADS = D_MODEL // 128
MLA_Q_RANK = 448
MLA_KV_RANK = 512
MLA_NOPE = 128
MLA_ROPE = 64
MLA_V = 128
MLA_QK = MLA_NOPE + MLA_ROPE

kernel_name = "hybrid_gmlp_diffattn_mla_dit"


def rms_norm(x, g):
    xf = x.astype(jnp.float32)
    y = xf * lax.rsqrt(jnp.mean(xf * xf, axis=-1, keepdims=True) + EPS)
    return (y * g.astype(jnp.float32)).astype(x.dtype)


def layer_norm(x, g):
    xf = x.astype(jnp.float32)
    xc = xf - jnp.mean(xf, axis=-1, keepdims=True)
    y = xc * lax.rsqrt(jnp.mean(xc * xc, axis=-1, keepdims=True) + EPS)
    return (y * g.astype(jnp.float32)).astype(x.dtype)


def modulate(h, shift, scale):
    return h * (1 + scale) + shift


def axial_rope_tables(rows, cols, rot_dim):
    a = rot_dim // 2
    inv = ROPE_BASE ** (-jnp.arange(0, a, 2, dtype=jnp.float32) / a)

    def axis_angles(pos):
        ang = pos.astype(jnp.float32)[:, None] * inv[None, :]
        return jnp.concatenate([ang, ang], axis=-1)

    ang = jnp.concatenate([axis_angles(rows), axis_angles(cols)], axis=-1)
    return jnp.cos(ang), jnp.sin(ang)


def _rotate_half(t):
    half = t.shape[-1] // 2
    return jnp.concatenate([-t[..., half:], t[..., :half]], axis=-1)


def apply_axial_rope(x, cos, sin):
    a = x.shape[-1] // 2
    xr = jnp.concatenate([_rotate_half(x[..., :a]), _rotate_half(x[..., a:])], axis=-1)
    bshape = (1, cos.shape[0]) + (1,) * (x.ndim - 3) + (cos.shape[-1],)
    return x * cos.reshape(bshape).astype(x.dtype) + xr * sin.reshape(bshape).astype(x.dtype)


def map_query_blocks(fn, q):
    B, S = q.shape[:2]
    nb = S // Q_BLOCK
    qb = jnp.moveaxis(q.reshape((B, nb, Q_BLOCK) + q.shape[2:]), 1, 0)
    out = jnp.moveaxis(lax.map(fn, qb), 0, 1)
    return out.reshape((B, S) + out.shape[3:])


def diff_attend(q, k, v, lam):
    scale = DIFF_HEAD_DIM ** -0.5

    def one(qb):
        s = jnp.einsum('bqhjd,bkhjd->bhjqk', qb, k, preferred_element_type=jnp.float32) * scale
        p = jax.nn.softmax(s, axis=-1)
        w = p[:, :, 0] - lam * p[:, :, 1]
        return jnp.einsum('bhqk,bkhe->bqhe', w.astype(v.dtype), v)

    return map_query_blocks(one, q)


def softmax_attend(q, k, v, scale):
    def one(qb):
        s = jnp.einsum('bqhd,bkhd->bhqk', qb, k, preferred_element_type=jnp.float32) * scale
        p = jax.nn.softmax(s, axis=-1)
        return jnp.einsum('bhqk,bkhe->bqhe', p.astype(v.dtype), v)

    return map_query_blocks(one, q)


def gmlp_chunk_mixer(h, w_in, b_in, ln_g, w_s, b_s, w_out):
    B, T, _ = h.shape
    z = jax.nn.gelu(h @ w_in + b_in)
    u, v = jnp.split(z, 2, axis=-1)
    v = layer_norm(v, ln_g)
    v = v.reshape(B, T // CHUNK, CHUNK, GMLP_GROUPS, GMLP_GROUP_DIM)
    sv = jnp.einsum('gpq,bnqgc->bnpgc', w_s, v) + b_s.T[:, :, None]
    return (u * sv.reshape(B, T, GMLP_WIDTH)) @ w_out


def diff_attention_mixer(h_lat, h_ctx, cos, sin, w_qkv, lam_q1, lam_k1, lam_q2, lam_k2,
                         subln_g, w_o, lambda_init, need_ctx_out):
    H, d = DIFF_HEADS, DIFF_HEAD_DIM

    def proj(h):
        B, T = h.shape[:2]
        q, k, v = jnp.split(h @ w_qkv, [2 * H * d, 4 * H * d], axis=-1)
        return q.reshape(B, T, H, 2, d), k.reshape(B, T, H, 2, d), v.reshape(B, T, H, DIFF_V_DIM)

    def post(o):
        B, T = o.shape[:2]
        o = rms_norm(o, subln_g) * (1.0 - lambda_init)
        return o.reshape(B, T, H * DIFF_V_DIM) @ w_o

    f32 = jnp.float32
    lam = (jnp.exp(jnp.sum(lam_q1.astype(f32) * lam_k1.astype(f32)))
           - jnp.exp(jnp.sum(lam_q2.astype(f32) * lam_k2.astype(f32))) + lambda_init)
    q_l, k_l, v_l = proj(h_lat)
    q_l = apply_axial_rope(q_l, cos, sin)
    k_l = apply_axial_rope(k_l, cos, sin)
    q_c, k_c, v_c = proj(h_ctx)
    k_all = jnp.concatenate([k_l, k_c], axis=1)
    v_all = jnp.concatenate([v_l, v_c], axis=1)
    y_lat = post(diff_attend(q_l, k_all, v_all, lam))
    y_ctx = post(diff_attend(q_c, k_c, v_c, lam)) if need_ctx_out else None
    return y_lat, y_ctx


def mla_mixer(h_lat, h_ctx, cos, sin, w_dqkv, q_norm_g, w_uq, kv_norm_g, w_ukv, w_o, need_ctx_out):
    H = MLA_HEADS

    def proj(h, rotary):
        B, T = h.shape[:2]
        c_q, c_kv, k_r = jnp.split(h @ w_dqkv, [MLA_Q_RANK, MLA_Q_RANK + MLA_KV_RANK], axis=-1)
        q = (rms_norm(c_q, q_norm_g) @ w_uq).reshape(B, T, H, MLA_QK)
        kv = (rms_norm(c_kv, kv_norm_g) @ w_ukv).reshape(B, T, H, MLA_NOPE + MLA_V)
        q_n, q_r = jnp.split(q, [MLA_NOPE], axis=-1)
        k_n, v = jnp.split(kv, [MLA_NOPE], axis=-1)
        k_r = k_r[:, :, None, :]
        if rotary:
            q_r = apply_axial_rope(q_r, cos, sin)
            k_r = apply_axial_rope(k_r, cos, sin)
        q = jnp.concatenate([q_n, q_r], axis=-1)
        k = jnp.concatenate([k_n, jnp.broadcast_to(k_r, (B, T, H, MLA_ROPE))], axis=-1)
        return q, k, v

    def post(o):
        B, T = o.shape[:2]
        return o.reshape(B, T, H * MLA_V) @ w_o

    scale = MLA_QK ** -0.5
    q_l, k_l, v_l = proj(h_lat, True)
    q_c, k_c, v_c = proj(h_ctx, False)
    k_all = jnp.concatenate([k_l, k_c], axis=1)
    v_all = jnp.concatenate([v_l, v_c], axis=1)
    y_lat = post(softmax_attend(q_l, k_all, v_all, scale))
    y_ctx = post(softmax_attend(q_c, k_c, v_c, scale)) if need_ctx_out else None
    return y_lat, y_ctx


def sq_relu_mlp(h, w1, w2):
    return jnp.square(jax.nn.relu(h @ w1)) @ w2


def setup_inputs(seed: int = 0) -> dict:
    key = jax.random.key(seed)
    ks = iter(jax.random.split(key, 32))

    def nrm(shape, scale):
        return jax.random.normal(next(ks), shape, jnp.float32) * scale

    def gain(shape):
        return 1.0 + nrm(shape, 0.02)

    D = D_MODEL
    return {
        "x": nrm((BATCH, SEQ, D), 1.0),
        "c": nrm((BATCH, D), 1.0),
        "ctx": nrm((BATCH, CTX_LEN, D), 1.0),
        "c_ctx": nrm((D,), 1.0),
        "w_mod": nrm((DEPTH, D, N_MOD * D), 0.5 * D ** -0.5),
        "b_mod": nrm((DEPTH, N_MOD * D), 0.02),
        "norm1_g": gain((DEPTH, D)),
        "norm2_g": gain((DEPTH, D)),
        "w_ff1": nrm((DEPTH, D, D_FF), D ** -0.5),
        "w_ff2": nrm((DEPTH, D_FF, D), D_FF ** -0.5),
        "a_w_in": nrm((N_A, D, 2 * GMLP_WIDTH), D ** -0.5),
        "a_b_in": nrm((N_A, 2 * GMLP_WIDTH), 0.02),
        "a_ln_g": gain((N_A, GMLP_WIDTH)),
        "a_w_s": nrm((N_A, GMLP_GROUPS, CHUNK, CHUNK), CHUNK ** -0.5),
        "a_b_s": gain((N_A, GMLP_GROUPS, CHUNK)),
        "a_w_out": nrm((N_A, GMLP_WIDTH, D), GMLP_WIDTH ** -0.5),
        "b_w_qkv": nrm((N_B, D, 4 * DIFF_HEADS * DIFF_HEAD_DIM + DIFF_HEADS * DIFF_V_DIM), D ** -0.5),
        "b_lam_q1": nrm((N_B, DIFF_HEAD_DIM), 0.1),
        "b_lam_k1": nrm((N_B, DIFF_HEAD_DIM), 0.1),
        "b_lam_q2": nrm((N_B, DIFF_HEAD_DIM), 0.1),
        "b_lam_k2": nrm((N_B, DIFF_HEAD_DIM), 0.1),
        "b_subln_g": gain((N_B, DIFF_V_DIM)),
        "b_w_o": nrm((N_B, DIFF_HEADS * DIFF_V_DIM, D), (DIFF_HEADS * DIFF_V_DIM) ** -0.5),
        "c_w_dqkv": nrm((N_C, D, MLA_Q_RANK + MLA_KV_RANK + MLA_ROPE), D ** -0.5),
        "c_q_norm_g": gain((N_C, MLA_Q_RANK)),
        "c_w_uq": nrm((N_C, MLA_Q_RANK, MLA_HEADS * MLA_QK), MLA_Q_RANK ** -0.5),
        "c_kv_norm_g": gain((N_C, MLA_KV_RANK)),
        "c_w_ukv": nrm((N_C, MLA_KV_RANK, MLA_HEADS * (MLA_NOPE + MLA_V)), MLA_KV_RANK ** -0.5),
        "c_w_o": nrm((N_C, MLA_HEADS * MLA_V, D), (MLA_HEADS * MLA_V) ** -0.5),
        "final_g": gain((D,)),
    }


def reference(x, c, ctx, c_ctx, w_mod, b_mod, norm1_g, norm2_g, w_ff1, w_ff2,
              a_w_in, a_b_in, a_ln_g, a_w_s, a_b_s, a_w_out,
              b_w_qkv, b_lam_q1, b_lam_k1, b_lam_q2, b_lam_k2, b_subln_g, b_w_o,
              c_w_dqkv, c_q_norm_g, c_w_uq, c_kv_norm_g, c_w_ukv, c_w_o, final_g):
    B, S, D = x.shape
    ROWS = S // GRID_W
    rows = jnp.repeat(jnp.arange(ROWS, dtype=jnp.int32), GRID_W)
    cols = jnp.tile(jnp.arange(GRID_W, dtype=jnp.int32), ROWS)
    cos_b, sin_b = axial_rope_tables(rows, cols, DIFF_HEAD_DIM)
    cos_c, sin_c = axial_rope_tables(rows, cols, MLA_ROPE)
    s_c = jax.nn.silu(c)
    s_cc = jax.nn.silu(c_ctx)
    xc = ctx

    for l in range(DEPTH):
        kind = l % N_MIXERS
        idx = l // N_MIXERS
        ctx_later = any(j % N_MIXERS != 0 for j in range(l + 1, DEPTH))
        ctx_here = ctx_later or kind != 0

        sh1, sc1, g1, sh2, sc2, g2 = jnp.split((s_c @ w_mod[l] + b_mod[l])[:, None, :], N_MOD, axis=-1)
        hx = modulate(rms_norm(x, norm1_g[l]), sh1, sc1)
        hc = None
        if ctx_here:
            csh1, csc1, cg1, csh2, csc2, cg2 = jnp.split(s_cc @ w_mod[l] + b_mod[l], N_MOD, axis=-1)
            hc = modulate(rms_norm(xc, norm1_g[l]), csh1, csc1)

        if kind == 0:
            p = (a_w_in[idx], a_b_in[idx], a_ln_g[idx], a_w_s[idx], a_b_s[idx], a_w_out[idx])
            y_x = gmlp_chunk_mixer(hx, *p)
            y_c = gmlp_chunk_mixer(hc, *p) if ctx_later else None
        elif kind == 1:
            lambda_init = 0.8 - 0.6 * math.exp(-0.3 * l)
            y_x, y_c = diff_attention_mixer(hx, hc, cos_b, sin_b, b_w_qkv[idx], b_lam_q1[idx], b_lam_k1[idx],
                                            b_lam_q2[idx], b_lam_k2[idx], b_subln_g[idx], b_w_o[idx],
                                            lambda_init, ctx_later)
        else:
            y_x, y_c = mla_mixer(hx, hc, cos_c, sin_c, c_w_dqkv[idx], c_q_norm_g[idx], c_w_uq[idx],
                                 c_kv_norm_g[idx], c_w_ukv[idx], c_w_o[idx], ctx_later)

        x = x + g1 * y_x
        x = x + g2 * sq_relu_mlp(modulate(rms_norm(x, norm2_g[l]), sh2, sc2), w_ff1[l], w_ff2[l])
        if ctx_later:
            xc = xc + cg1 * y_c
            xc = xc + cg2 * sq_relu_mlp(modulate(rms_norm(xc, norm2_g[l]), csh2, csc2), w_ff1[l], w_ff2[l])

    return rms_norm(x, final_g)
```

```python
import math
from contextlib import ExitStack
import numpy as np
import ml_dtypes
import concourse.bass as bass
import concourse.mybir as mybir
from concourse.bass_utils import run_bass_kernel_spmd

F32 = mybir.dt.float32
BF16 = mybir.dt.bfloat16
AF = mybir.ActivationFunctionType
ALU = mybir.AluOpType
NPBF16 = ml_dtypes.bfloat16

D = 2048
KC = 16
SEQ = 8192
CTX = 256
TT = 768
EPS = 1e-6
N_DMA_SEMS = 16
DEBUG_MODE = ""
STREAMS = ("pe", "act", "dve", "pool", "sp")


class Res:
    __slots__ = ("name", "writers", "readers", "dma_readers", "excl")

    def __init__(self, name="", excl=False):
        self.name = name
        self.excl = excl
        self.writers = []
        self.readers = {}
        self.dma_readers = []


class Op:
    __slots__ = ("st", "fn", "deps", "signal", "tick", "is_dma", "dsem", "dval", "prewait")

    def __init__(self, st, fn, is_dma):
        self.st = st
        self.fn = fn
        self.is_dma = is_dma
        self.deps = []
        self.signal = False
        self.tick = 0
        self.dsem = None
        self.dval = 0
        self.prewait = None


class _Rec:
    def __init__(self):
        self.call = None

    def __getattr__(self, name):
        def f(*a, **k):
            self.call = (name, a, k)
            return self
        return f


def _flat(rs):
    out = []
    for r in rs:
        if isinstance(r, (list, tuple)):
            out.extend(_flat(r))
        else:
            out.append(r)
    return out


class Prog:
    def __init__(self, nc):
        self.nc = nc
        self.ops = {s: [] for s in STREAMS}
        self.n_dma = {s: 0 for s in STREAMS}
        self.dma_streams = set()

    def add(self, st, fn, reads=(), writes=(), dma=False):
        rec = _Rec()
        fn(rec)
        op = Op(st, rec.call, dma)
        reads = _flat(reads)
        writes = _flat(writes)
        deps = {}
        for r in reads:
            for w in r.writers:
                deps[id(w)] = (w, True)
            if r.excl:
                for ost, o in r.readers.items():
                    if ost != st and id(o) not in deps:
                        deps[id(o)] = (o, False)
        join = {}
        for r in writes:
            has_readers = bool(r.readers) or bool(r.dma_readers)
            j = dma and (not has_readers) and len(r.writers) > 0 and all(w.is_dma for w in r.writers)
            join[id(r)] = j
            if not j:
                for w in r.writers:
                    if id(w) not in deps:
                        deps[id(w)] = (w, False)
            for o in r.readers.values():
                if id(o) not in deps:
                    deps[id(o)] = (o, False)
            for o in r.dma_readers:
                if id(o) not in deps:
                    deps[id(o)] = (o, False)
        for d, raw in deps.values():
            if d is op:
                continue
            if (not d.is_dma) and (not dma) and d.st == st:
                if st == "pe" or not raw:
                    continue
            if not d.is_dma:
                d.signal = True
            op.deps.append(d)
        for r in reads:
            if dma:
                r.dma_readers.append(op)
            else:
                r.readers[st] = op
        for r in writes:
            if join[id(r)]:
                r.writers.append(op)
            else:
                r.writers = [op]
                r.readers = {}
                r.dma_readers = []
        if dma:
            self.dma_streams.add(st)
            k = self.n_dma[st]
            self.n_dma[st] = k + 1
            op.dsem = k % N_DMA_SEMS
            op.dval = 16 * (k // N_DMA_SEMS + 1)
            if k >= N_DMA_SEMS:
                op.prewait = (op.dsem, 16 * (k // N_DMA_SEMS))
        self.ops[st].append(op)
        return op

    def emit(self):
        nc = self.nc
        with ExitStack() as es:
            csem = {}
            for s in ("pe", "act", "dve", "pool"):
                csem[s] = es.enter_context(nc.semaphore("c_" + s))
            dsem = {}
            for s in sorted(self.dma_streams):
                dsem[s] = [es.enter_context(nc.semaphore("d_%s_%d" % (s, i)))
                           for i in range(min(N_DMA_SEMS, self.n_dma[s]))]
            for s in STREAMS:
                t = 0
                for op in self.ops[s]:
                    if (not op.is_dma) and op.signal:
                        t += 1
                        op.tick = t
            block = es.enter_context(nc.Block())
            prog = self

            def run_stream(s, eng):
                waited = {}
                for op in prog.ops[s]:
                    need = {}
                    for d in op.deps:
                        if d.is_dma:
                            key = ("d", d.st, d.dsem)
                            sem = dsem[d.st][d.dsem]
                            val = d.dval
                        else:
                            key = ("c", d.st)
                            sem = csem[d.st]
                            val = d.tick
                        if need.get(key, (None, -1))[1] < val:
                            need[key] = (sem, val)
                    if op.prewait is not None:
                        key = ("d", s, op.prewait[0])
                        if need.get(key, (None, -1))[1] < op.prewait[1]:
                            need[key] = (dsem[s][op.prewait[0]], op.prewait[1])
                    for key, (sem, val) in need.items():
                        if waited.get(key, -1) >= val:
                            continue
                        waited[key] = val
                        eng.wait_ge(sem, val)
                    name_, a_, k_ = op.fn
                    inst = getattr(eng, name_)(*a_, **k_)
                    if op.is_dma:
                        inst.then_inc(dsem[s][op.dsem], 16)
                    elif op.signal:
                        inst.then_inc(csem[s], 1)
                if s == "sp":
                    for ds in sorted(prog.dma_streams):
                        n = prog.n_dma[ds]
                        for i in range(min(N_DMA_SEMS, n)):
                            cnt = (n - i + N_DMA_SEMS - 1) // N_DMA_SEMS
                            eng.wait_ge(dsem[ds][i], 16 * cnt)

            @block.tensor
            def _(e):
                run_stream("pe", e)

            @block.scalar
            def _(e):
                run_stream("act", e)

            @block.vector
            def _(e):
                run_stream("dve", e)

            @block.gpsimd
            def _(e):
                run_stream("pool", e)

            @block.sync
            def _(e):
                run_stream("sp", e)


class Rot:
    def __init__(self, tiles):
        self.tiles = tiles
        self.res = [Res() for _ in tiles]
        self.i = 0

    def next(self):
        j = self.i % len(self.tiles)
        self.i += 1
        return self.tiles[j], self.res[j]


class Builder:
    def __init__(self):
        self.nc = bass.Bass("TRN2", target_bir_lowering=False)
        self.P = Prog(self.nc)
        self.es = ExitStack()
        self.es.__enter__()
        self.in_names = []
        self.out_names = []
        self._n = 0

    def inp(self, name, shape, dt=F32):
        self.in_names.append(name)
        return self.nc.dram_tensor(name, list(shape), dt, kind="ExternalInput").ap()

    def out(self, name, shape, dt=F32):
        self.out_names.append(name)
        return self.nc.dram_tensor(name, list(shape), dt, kind="ExternalOutput").ap()

    def scratch(self, name, shape, dt=F32):
        return self.nc.dram_tensor(name, list(shape), dt, kind="Internal").ap()

    def sb(self, shape, dt, name=None):
        self._n += 1
        return self.es.enter_context(self.nc.sbuf_tensor(name or ("sb%d" % self._n), list(shape), dt))

    def psum(self, shape, dt=F32, name=None):
        self._n += 1
        return self.es.enter_context(self.nc.psum_tensor(name or ("ps%d" % self._n), list(shape), dt))

    def rot_sb(self, n, shape, dt):
        return Rot([self.sb(shape, dt) for _ in range(n)])

    def rot_ps(self, n, shape):
        return Rot([self.psum(shape) for _ in range(n)])

    def finish(self):
        self.P.emit()
        self.es.__exit__(None, None, None)
        return self.nc

    def setup_common(self, n_w=4):
        P = self.P
        self.wslots = self.rot_sb(n_w, [128, 8192], BF16)
        self.ps2 = self.rot_ps(4, [128, 1024])
        self.bank = [Res("bank%d" % i, excl=True) for i in range(8)]
        self.ps2.res = [[self.bank[2 * i], self.bank[2 * i + 1]] for i in range(4)]
        self.ones = self.sb([128, 128], BF16)
        self.r_ones = Res("ones")
        P.add("dve", lambda e: e.memset(self.ones[:], 1.0), writes=[self.r_ones])
        self.epst = self.sb([128, 1], F32)
        P.add("dve", lambda e: e.memset(self.epst[:], EPS), writes=[self.r_ones])
        self.xk = self.rot_sb(3, [128, TT], F32)
        self.tmpf = self.rot_sb(3, [128, TT], F32)
        self.sqb = self.rot_sb(2, [128, TT], BF16)
        self.Rt = self.sb([128, TT], F32)
        self.r_R = Res("R")

    def wload(self, src, nk, ncols):
        t, r = self.wslots.next()
        view = t[:, 0:nk * ncols].rearrange("p (k c) -> p k c", k=nk)
        self.P.add("pool", lambda e: e.dma_start(out=view, in_=src), writes=[r], dma=True)
        return view, r

    def load_small(self, src, shape, dt=F32, cast=False):
        t = self.sb(shape, dt)
        r = Res()
        self.P.add("pool" if cast else "sp", lambda e: e.dma_start(out=t[:], in_=src), writes=[r], dma=True)
        return t, r


def wview(w, k0, nk, c0, ncols):
    return w[k0 * 128:(k0 + nk) * 128, c0:c0 + ncols].rearrange("(k p) c -> p k c", p=128)


def _aslist(r):
    return list(r) if isinstance(r, (list, tuple)) else [r]


def tiles_for(with_ctx):
    t = [(0, [(0, 512, 0), (512, 256, 0)]), (768, [(0, 512, 0), (512, 256, 0)])]
    if with_ctx:
        t.append((1536, [(0, 512, 0), (512, 256, 1)]))
    else:
        t.append((1536, [(0, 512, 0)]))
    return t


def tile_n(tile):
    return sum(s[1] for s in tile[1])


def emit_modprep(b, modv_d, ng_d, j_shift, j_scale, j_gate):
    P = b.P
    modv, r_mod = b.load_small(modv_d, [128, 96, 2])
    ng, r_ng = b.load_small(ng_d, [128, 16])
    A = b.sb([128, 16, 2], F32)
    r_A = Res("A")
    for w in range(2):
        P.add("dve", lambda e, w=w: e.scalar_tensor_tensor(
            out=A[:, :, w], in0=modv[:, j_scale * 16:(j_scale + 1) * 16, w], scalar=1.0, in1=ng[:, :],
            op0=ALU.add, op1=ALU.mult), reads=[r_mod, r_ng], writes=[r_A])
    Bv = modv[:, j_shift * 16:(j_shift + 1) * 16, :]
    G = modv[:, j_gate * 16:(j_gate + 1) * 16, :]
    b.last_modv = (modv, r_mod)
    return (A, r_A), (Bv, r_mod), (G, r_mod)


def emit_rms_stats(b, x_d, tile, nfeat_chunks=KC, inv_n=1.0 / D):
    P = b.P
    tok0, segs = tile
    n = tile_n(tile)
    ps, r_ps = b.ps2.next()
    for k in range(nfeat_chunks):
        xk, r_xk = b.xk.next()
        P.add("sp", lambda e, xk=xk, k=k: e.dma_start(out=xk[:, 0:n], in_=x_d[k * 128:(k + 1) * 128, tok0:tok0 + n]),
              writes=[r_xk], dma=True)
        sq, r_sq = b.sqb.next()
        P.add("act", lambda e, xk=xk, sq=sq: e.activation(out=sq[:, 0:n], in_=xk[:, 0:n], func=AF.Square),
              reads=[r_xk], writes=[r_sq])
        for (c0, cn, _) in segs:
            P.add("pe", lambda e, sq=sq, c0=c0, cn=cn, k=k, ps=ps: e.matmul(
                ps[:, c0:c0 + cn], lhsT=b.ones[:, :], rhs=sq[:, c0:c0 + cn],
                start=(k == 0), stop=(k == nfeat_chunks - 1)),
                reads=[r_sq, b.r_ones], writes=[r_ps])
    R = b.Rt
    P.add("act", lambda e, ps=ps: e.activation(out=R[:, 0:n], in_=ps[:, 0:n], func=AF.Sqrt, scale=inv_n,
                                               bias=b.epst[:, 0:1]), reads=[r_ps, b.r_ones], writes=[b.r_R])
    P.add("dve", lambda e: e.reciprocal(out=R[:, 0:n], in_=R[:, 0:n]), reads=[b.r_R], writes=[b.r_R])


def emit_norm_mod(b, x_d, tile, A, Bv, hT, r_h):
    P = b.P
    tok0, segs = tile
    n = tile_n(tile)
    emit_rms_stats(b, x_d, tile)
    (At, r_A), (Bt, r_B) = A, Bv
    for k in range(KC):
        xk, r_xk = b.xk.next()
        P.add("sp", lambda e, xk=xk, k=k: e.dma_start(out=xk[:, 0:n], in_=x_d[k * 128:(k + 1) * 128, tok0:tok0 + n]),
              writes=[r_xk], dma=True)
        tm, r_tm = b.tmpf.next()
        P.add("dve", lambda e, xk=xk, tm=tm: e.tensor_tensor(out=tm[:, 0:n], in0=xk[:, 0:n], in1=b.Rt[:, 0:n],
                                                             op=ALU.mult), reads=[r_xk, b.r_R], writes=[r_tm])
        for (c0, cn, w) in segs:
            P.add("act", lambda e, tm=tm, c0=c0, cn=cn, w=w, k=k: e.activation(
                out=hT[:, k, c0:c0 + cn], in_=tm[:, c0:c0 + cn], func=AF.Identity,
                scale=At[:, k, w:w + 1], bias=Bt[:, k, w:w + 1]),
                reads=[r_tm, r_A, r_B], writes=[r_h])


def emit_linear_fm(b, w_d, k0, nk, c0, nchunks, rhsT, r_rhs, tile, evac, blk_chunks=4):
    P = b.P
    tok0, segs = tile
    ncols_blk = blk_chunks * 128
    assert nk * ncols_blk <= 8192
    nblk = (nchunks + blk_chunks - 1) // blk_chunks
    for blk in range(nblk):
        nch = min(blk_chunks, nchunks - blk * blk_chunks)
        wv, r_w = b.wload(wview(w_d, k0, nk, c0 + blk * ncols_blk, nch * 128), nk, nch * 128)
        for j in range(nch):
            m = blk * blk_chunks + j
            ps, r_ps = b.ps2.next()
            for k in range(nk):
                for (s0, sn, _) in segs:
                    P.add("pe", lambda e, wv=wv, k=k, j=j, s0=s0, sn=sn, ps=ps: e.matmul(
                        ps[:, s0:s0 + sn], lhsT=wv[:, k, j * 128:(j + 1) * 128], rhs=rhsT[:, k, s0:s0 + sn],
                        start=(k == 0), stop=(k == nk - 1)), reads=[r_w] + _aslist(r_rhs), writes=[r_ps])
            evac(m, ps, r_ps)


def emit_residual_evac(b, xin_d, xout_d, r_xout, tile, G, m, ps, r_ps):
    P = b.P
    tok0, segs = tile
    n = tile_n(tile)
    (Gt, r_G) = G
    xk, r_xk = b.xk.next()
    rd = [r_xout[m]] if xin_d is xout_d else []
    P.add("sp", lambda e: e.dma_start(out=xk[:, 0:n], in_=xin_d[m * 128:(m + 1) * 128, tok0:tok0 + n]),
          reads=rd, writes=[r_xk], dma=True)
    o, r_o = b.tmpf.next()
    for (c0, cn, w) in segs:
        P.add("dve", lambda e, c0=c0, cn=cn, w=w: e.scalar_tensor_tensor(
            out=o[:, c0:c0 + cn], in0=ps[:, c0:c0 + cn], scalar=Gt[:, m, w:w + 1], in1=xk[:, c0:c0 + cn],
            op0=ALU.mult, op1=ALU.add), reads=[r_ps, r_xk, r_G], writes=[r_o])
    P.add("sp", lambda e: e.dma_start(out=xout_d[m * 128:(m + 1) * 128, tok0:tok0 + n], in_=o[:, 0:n]),
          reads=[r_o], writes=[r_xout[m]], dma=True)


def emit_ffn(b, tile, hT, r_h, hid, r_hid, xin_d, xout_d, r_xout, w1_d, w2_d, G):
    P = b.P
    n = tile_n(tile)
    for half in range(2):
        def evac1(m, ps, r_ps):
            tm, r_tm = b.tmpf.next()
            P.add("act", lambda e: e.activation(out=tm[:, 0:n], in_=ps[:, 0:n], func=AF.Relu),
                  reads=[r_ps], writes=[r_tm])
            P.add("dve", lambda e: e.tensor_tensor(out=hid[:, m, 0:n], in0=tm[:, 0:n], in1=tm[:, 0:n], op=ALU.mult),
                  reads=[r_tm], writes=_aslist(r_hid))
        emit_linear_fm(b, w1_d, 0, KC, half * 4096, 32, hT, r_h, tile, evac1, blk_chunks=4)

        def evac2(m, ps, r_ps, half=half):
            emit_residual_evac(b, xin_d if half == 0 else xout_d, xout_d, r_xout, tile, G, m, ps, r_ps)
        emit_linear_fm(b, w2_d, half * 32, 32, 0, KC, hid, r_hid, tile, evac2, blk_chunks=2)


def emit_final_norm(b, x_d, out_d, tile, fg, r_fg):
    P = b.P
    tok0, segs = tile
    n = tile_n(tile)
    emit_rms_stats(b, x_d, tile)
    for k in range(KC):
        xk, r_xk = b.xk.next()
        P.add("sp", lambda e, xk=xk, k=k: e.dma_start(out=xk[:, 0:n], in_=x_d[k * 128:(k + 1) * 128, tok0:tok0 + n]),
              writes=[r_xk], dma=True)
        tm, r_tm = b.tmpf.next()
        P.add("dve", lambda e, xk=xk, tm=tm, k=k: e.scalar_tensor_tensor(
            out=tm[:, 0:n], in0=xk[:, 0:n], scalar=fg[:, k:k + 1], in1=b.Rt[:, 0:n], op0=ALU.mult, op1=ALU.mult),
            reads=[r_xk, b.r_R, r_fg], writes=[r_tm])
        P.add("sp", lambda e, tm=tm, k=k: e.dma_start(out=out_d[k * 128:(k + 1) * 128, tok0:tok0 + n], in_=tm[:, 0:n]),
              reads=[r_tm], dma=True)


def emit_gmlp(b, tile, hT, r_h, uT, r_u, vt, r_v, xin_d, xout_d, r_xout, W, G):
    P = b.P
    tok0, segs = tile
    n = tile_n(tile)
    ntb = n // 128
    w_in = W["w_in"]

    def evac_u(m, ps, r_ps):
        P.add("act", lambda e: e.activation(out=uT[:, m, 0:n], in_=ps[:, 0:n], func=AF.Gelu_apprx_tanh,
                                            bias=W["b_u"][:, m:m + 1], scale=1.0),
              reads=[r_ps, W["r_c"]], writes=[r_u])
    emit_linear_fm(b, w_in, 0, KC, 0, 16, hT, r_h, tile, evac_u, blk_chunks=4)

    ssum = b.sb([128, 6, 4], F32)
    ssq = b.sb([128, 6, 4], F32)
    r_st = Res("stats")
    for nb in range(4):
        wv, r_w = b.wload(wview(w_in, 0, KC, 2048 + nb * 512, 512), KC, 512)
        for tb in range(ntb):
            ps, r_ps = b.ps2.next()
            for k in range(KC):
                P.add("pe", lambda e, wv=wv, k=k, tb=tb, ps=ps: e.matmul(
                    ps[:, 0:512], lhsT=hT[:, k, tb * 128:(tb + 1) * 128], rhs=wv[:, k, :],
                    start=(k == 0), stop=(k == KC - 1)), reads=[r_w, r_h], writes=[r_ps])
            tm, r_tm = b.tmpf.next()
            P.add("dve", lambda e, ps=ps, tm=tm, nb=nb: e.tensor_tensor(
                out=tm[:, 0:512], in0=ps[:, 0:512], in1=W["b_v"][:, nb * 512:(nb + 1) * 512], op=ALU.add),
                reads=[r_ps, W["r_c"]], writes=[r_tm])
            P.add("act", lambda e, tm=tm, tb=tb, nb=nb: e.activation(
                out=vt[:, tb, nb * 512:(nb + 1) * 512], in_=tm[:, 0:512], func=AF.Gelu_apprx_tanh,
                accum_out=ssum[:, tb, nb:nb + 1]), reads=[r_tm], writes=[r_v, r_st])
            sq, r_sq = b.sqb.next()
            P.add("act", lambda e, sq=sq, tb=tb, nb=nb: e.activation(
                out=sq[:, 0:512], in_=vt[:, tb, nb * 512:(nb + 1) * 512], func=AF.Square,
                accum_out=ssq[:, tb, nb:nb + 1]), reads=[r_v], writes=[r_sq, r_st])
    mean = b.sb([128, 6], F32)
    ex2 = b.sb([128, 6], F32)
    rstd = b.sb([128, 6], F32)

    def add4(dst, src):
        P.add("dve", lambda e: e.tensor_tensor(out=dst[:, 0:ntb], in0=src[:, 0:ntb, 0], in1=src[:, 0:ntb, 1], op=ALU.add),
              reads=[r_st], writes=[r_st])
        for j in (2, 3):
            P.add("dve", lambda e, j=j: e.tensor_tensor(out=dst[:, 0:ntb], in0=dst[:, 0:ntb], in1=src[:, 0:ntb, j], op=ALU.add),
                  reads=[r_st], writes=[r_st])
    add4(mean, ssum)
    add4(ex2, ssq)
    P.add("dve", lambda e: e.tensor_scalar(out=mean[:, 0:ntb], in0=mean[:, 0:ntb], scalar1=1.0 / 2048, scalar2=None, op0=ALU.mult),
          reads=[r_st], writes=[r_st])
    P.add("dve", lambda e: e.tensor_tensor(out=rstd[:, 0:ntb], in0=mean[:, 0:ntb], in1=mean[:, 0:ntb], op=ALU.mult),
          reads=[r_st], writes=[r_st])
    P.add("dve", lambda e: e.scalar_tensor_tensor(out=rstd[:, 0:ntb], in0=ex2[:, 0:ntb], scalar=1.0 / 2048, in1=rstd[:, 0:ntb],
                                                  op0=ALU.mult, op1=ALU.subtract), reads=[r_st], writes=[r_st])
    P.add("act", lambda e: e.activation(out=rstd[:, 0:ntb], in_=rstd[:, 0:ntb], func=AF.Sqrt, scale=1.0,
                                        bias=b.epst[:, 0:1]), reads=[r_st, b.r_ones], writes=[r_st])
    P.add("dve", lambda e: e.reciprocal(out=rstd[:, 0:ntb], in_=rstd[:, 0:ntb]), reads=[r_st], writes=[r_st])
    for tb in range(ntb):
        P.add("dve", lambda e, tb=tb: e.tensor_scalar(
            out=vt[:, tb, :], in0=vt[:, tb, :], scalar1=mean[:, tb:tb + 1], scalar2=rstd[:, tb:tb + 1],
            op0=ALU.subtract, op1=ALU.mult), reads=[r_v, r_st], writes=[r_v])
    for g in range(16):
        ps, r_ps = b.ps2.next()
        for tb in range(ntb):
            P.add("pe", lambda e, g=g, tb=tb, ps=ps: e.matmul(
                ps[:, tb * 128:(tb + 1) * 128], lhsT=vt[:, tb, g * 128:(g + 1) * 128], rhs=W["wsT"][:, g, :],
                start=True, stop=True), reads=[r_v, W["r_c"]], writes=[r_ps])
        tm, r_tm = b.tmpf.next()
        P.add("dve", lambda e, g=g, ps=ps, tm=tm: e.scalar_tensor_tensor(
            out=tm[:, 0:n].rearrange("p (a c) -> p a c", c=128),
            in0=ps[:, 0:n].rearrange("p (a c) -> p a c", c=128),
            scalar=W["ln_g"][:, g:g + 1],
            in1=W["b_s"][:, g:g + 1, :].broadcast_to([128, ntb, 128]),
            op0=ALU.mult, op1=ALU.add), reads=[r_ps, W["r_c"]], writes=[r_tm])
        P.add("dve", lambda e, g=g, tm=tm: e.tensor_tensor(out=uT[:, g, 0:n], in0=uT[:, g, 0:n], in1=tm[:, 0:n], op=ALU.mult),
              reads=[r_tm, r_u], writes=[r_u])

    def evac_o(m, ps, r_ps):
        emit_residual_evac(b, xin_d, xout_d, r_xout, tile, G, m, ps, r_ps)
    emit_linear_fm(b, W["w_out"], 0, KC, 0, 16, uT, r_u, tile, evac_o, blk_chunks=4)


def build_stage_mod():
    b = Builder()
    P = b.P
    wm = b.inp("wm", [4, D, 1536])
    bm = b.inp("bm", [128, 48])
    cT = b.inp("cT", [128, 16, 3])
    o = b.out("modo", [128, 48, 3])
    ct, r_ct = b.load_small(cT, [128, 16, 3])
    bt, r_bt = b.load_small(bm, [128, 48])
    sT = b.sb([128, 16, 3], F32)
    r_s = Res()
    P.add("act", lambda e: e.activation(out=sT[:], in_=ct[:], func=AF.Silu), reads=[r_ct], writes=[r_s])
    wsl = b.rot_sb(3, [128, 16, 512], F32)
    pss = b.rot_ps(4, [128, 512])
    ot = b.sb([128, 48, 3], F32)
    r_o = Res()
    for l in range(4):
        for blk in range(3):
            wt, r_w = wsl.next()
            P.add("sp", lambda e, wt=wt, l=l, blk=blk: e.dma_start(
                out=wt[:], in_=wm[l, :, blk * 512:(blk + 1) * 512].rearrange("(k p) c -> p k c", p=128)),
                writes=[r_w], dma=True)
            for j in range(4):
                mc = blk * 4 + j
                ps, r_ps = pss.next()
                for k in range(16):
                    P.add("pe", lambda e, wt=wt, k=k, j=j, ps=ps: e.matmul(
                        ps[:, 0:3], lhsT=wt[:, k, j * 128:(j + 1) * 128], rhs=sT[:, k, :],
                        start=(k == 0), stop=(k == 15)), reads=[r_w, r_s], writes=[r_ps])
                P.add("dve", lambda e, ps=ps, l=l, mc=mc: e.tensor_scalar(
                    out=ot[:, l * 12 + mc, :], in0=ps[:, 0:3], scalar1=bt[:, l * 12 + mc:l * 12 + mc + 1], scalar2=None,
                    op0=ALU.add), reads=[r_ps, r_bt], writes=[r_o])
    P.add("sp", lambda e: e.dma_start(out=o, in_=ot[:]), reads=[r_o], dma=True)
    return b


def build_stage_gmlp_layer(with_ctx, final):
    b = Builder()
    P = b.P
    ntok = 2304 if with_ctx else 2048
    xin = b.inp("xT", [D, ntok])
    modv = b.inp("modv", [128, 96, 2])
    n1g = b.inp("n1g", [128, 16])
    n2g = b.inp("n2g", [128, 16])
    w_in = b.inp("w_in", [D, 4096])
    b_u = b.inp("b_u", [128, 16])
    b_v = b.inp("b_v", [128, 2048])
    ln_g = b.inp("ln_g", [128, 16])
    wsT = b.inp("wsT", [128, 16, 128])
    b_s = b.inp("b_s", [128, 16, 128])
    w_out = b.inp("w_out", [D, D])
    w1 = b.inp("w1", [D, 8192])
    w2 = b.inp("w2", [8192, D])
    if final:
        fgd = b.inp("fg", [128, 16])
        xo2 = b.scratch("xo2", [D, ntok])
        outd = b.out("xo", [D, ntok])
    else:
        xo2 = b.out("xo", [D, ntok])
    xmid = b.scratch("xmid", [D, ntok])
    b.setup_common()
    W = {"w_in": w_in, "w_out": w_out, "r_c": Res("consts")}
    rc = W["r_c"]
    for nm, src, shape, dt, cast in (("b_u", b_u, [128, 16], F32, False), ("b_v", b_v, [128, 2048], F32, False),
                                     ("ln_g", ln_g, [128, 16], F32, False), ("wsT", wsT, [128, 16, 128], BF16, True),
                                     ("b_s", b_s, [128, 16, 128], F32, False)):
        t = b.sb(shape, dt)
        P.add("pool" if cast else "sp", lambda e, t=t, src=src: e.dma_start(out=t[:], in_=src), writes=[rc], dma=True)
        W[nm] = t
    A1, B1, G1 = emit_modprep(b, modv, n1g, 0, 1, 2)
    A2, B2, G2 = emit_modprep(b, modv, n2g, 3, 4, 5)
    if final:
        fg, r_fg = b.load_small(fgd, [128, 16])
    hT = b.sb([128, 16, TT], BF16)
    r_h = Res("hT")
    big = b.sb([128, 32 * TT], BF16)
    uT = big[:, 0:16 * TT].rearrange("p (k t) -> p k t", k=16)
    r_u = Res("uT")
    vt = big[:, 16 * TT:32 * TT].rearrange("p (a f) -> p a f", a=6)
    r_v = Res("vt")
    hid = big[:, :].rearrange("p (k t) -> p k t", k=32)
    r_hid = [r_u, r_v]
    for ti, tile in enumerate(tiles_for(with_ctx)):
        r_xmid = [Res() for _ in range(KC)]
        r_xo = [Res() for _ in range(KC)]
        emit_norm_mod(b, xin, tile, A1, B1, hT, r_h)
        emit_gmlp(b, tile, hT, r_h, uT, r_u, vt, r_v, xin, xmid, r_xmid, W, G1)
        emit_norm_mod_dep(b, xmid, r_xmid, tile, A2, B2, hT, r_h)
        emit_ffn(b, tile, hT, r_h, hid, r_hid, xmid, xo2, r_xo, w1, w2, G2)
        if final:
            emit_final_norm_dep(b, xo2, r_xo, outd, tile, fg, r_fg)
    return b


class _DepView:
    pass


def emit_norm_mod_dep(b, x_d, r_x, tile, A, Bv, hT, r_h):
    P = b.P
    orig = P.add

    def add(st, fn, reads=(), writes=(), dma=False):
        if dma and st == "sp":
            reads = list(reads) + list(r_x)
        return orig(st, fn, reads=reads, writes=writes, dma=dma)
    P.add = add
    try:
        emit_norm_mod(b, x_d, tile, A, Bv, hT, r_h)
    finally:
        P.add = orig


def emit_final_norm_dep(b, x_d, r_x, out_d, tile, fg, r_fg):
    P = b.P
    orig = P.add

    def add(st, fn, reads=(), writes=(), dma=False):
        if dma and st == "sp" and not reads:
            reads = list(r_x)
        return orig(st, fn, reads=reads, writes=writes, dma=dma)
    P.add = add
    try:
        emit_final_norm(b, x_d, out_d, tile, fg, r_fg)
    finally:
        P.add = orig


def _split512(n):
    out = []
    c = 0
    while c < n:
        out.append((c, min(512, n - c)))
        c += 512
    return out


def emit_rope_evac(b, ps, r_ps, n_lat, n, DR, prot, cosT, sinT, r_cs, dst_d, w_res=()):
    P = b.P
    ob, r_ob = b.obf.next()
    if n_lat > 0:
        qf, r_qf = b.tmpf.next()
        P.add("act", lambda e: e.activation(out=qf[0:DR, 0:n_lat], in_=ps[0:DR, 0:n_lat], func=AF.Identity),
              reads=[r_ps], writes=[r_qf])
        qb, r_qb = b.sqb.next()
        P.add("dve", lambda e: e.tensor_copy(out=qb[0:DR, 0:n_lat], in_=qf[0:DR, 0:n_lat]), reads=[r_qf], writes=[r_qb])
        pr, r_pr = b.ps2.next()
        for (s0, sn) in _split512(n_lat):
            P.add("pe", lambda e, s0=s0, sn=sn: e.matmul(pr[0:DR, s0:s0 + sn], lhsT=prot[0:DR, 0:DR], rhs=qb[0:DR, s0:s0 + sn],
                                                         start=True, stop=True), reads=[r_qb, r_cs], writes=[r_pr])
        P.add("dve", lambda e: e.tensor_tensor(out=qf[0:DR, 0:n_lat], in0=qf[0:DR, 0:n_lat], in1=cosT[0:DR, 0:n_lat], op=ALU.mult),
              reads=[r_qf, r_cs], writes=[r_qf])
        t2, r_t2 = b.tmpf.next()
        P.add("dve", lambda e: e.tensor_tensor(out=t2[0:DR, 0:n_lat], in0=pr[0:DR, 0:n_lat], in1=sinT[0:DR, 0:n_lat], op=ALU.mult),
              reads=[r_pr, r_cs], writes=[r_t2])
        P.add("dve", lambda e: e.tensor_tensor(out=ob[0:DR, 0:n_lat], in0=qf[0:DR, 0:n_lat], in1=t2[0:DR, 0:n_lat], op=ALU.add),
              reads=[r_qf, r_t2], writes=[r_ob])
    if n > n_lat:
        P.add("act", lambda e: e.activation(out=ob[0:DR, n_lat:n], in_=ps[0:DR, n_lat:n], func=AF.Identity),
              reads=[r_ps], writes=[r_ob])
    P.add("sp", lambda e: e.dma_start(out=dst_d, in_=ob[0:DR, 0:n]), reads=[r_ob], writes=list(w_res), dma=True)


def emit_copy_evac(b, ps, r_ps, rows, n, dst_d, w_res=()):
    P = b.P
    ob, r_ob = b.obf.next()
    P.add("act", lambda e: e.activation(out=ob[0:rows, 0:n], in_=ps[0:rows, 0:n], func=AF.Identity), reads=[r_ps], writes=[r_ob])
    P.add("sp", lambda e: e.dma_start(out=dst_d, in_=ob[0:rows, 0:n]), reads=[r_ob], writes=list(w_res), dma=True)


def n_latent(tile):
    return sum(s[1] for s in tile[1] if s[2] == 0)


def load_rope_tiles(b, cos_d, sin_d, prot_d, DR):
    cosT = b.sb([128, TT], F32)
    sinT = b.sb([128, TT], F32)
    prot = b.sb([128, 128], BF16)
    r_cs = Res("cs")
    b.P.add("pool", lambda e: e.dma_start(out=prot[0:DR, 0:DR], in_=prot_d), writes=[r_cs], dma=True)
    return cosT, sinT, prot, r_cs


def emit_load_cs(b, cosT, sinT, r_cs, cos_d, sin_d, DR, tok0, nl):
    b.P.add("sp", lambda e: e.dma_start(out=cosT[0:DR, 0:nl], in_=cos_d[:, tok0:tok0 + nl]), writes=[r_cs], dma=True)
    b.P.add("sp", lambda e: e.dma_start(out=sinT[0:DR, 0:nl], in_=sin_d[:, tok0:tok0 + nl]), writes=[r_cs], dma=True)


def emit_v_tokmajor(b, w_d, k0, nk, c0, ncols_total, lhsT_t, r_l, tile, V_d, vcol0):
    P = b.P
    tok0, segs = tile
    n = tile_n(tile)
    for nb in range(ncols_total // 512):
        wv, r_w = b.wload(wview(w_d, k0, nk, c0 + nb * 512, 512), nk, 512)
        for tb in range(n // 128):
            ps, r_ps = b.ps2.next()
            for k in range(nk):
                P.add("pe", lambda e, wv=wv, k=k, tb=tb, ps=ps: e.matmul(
                    ps[:, 0:512], lhsT=lhsT_t[:, k, tb * 128:(tb + 1) * 128], rhs=wv[:, k, :],
                    start=(k == 0), stop=(k == nk - 1)), reads=[r_w, r_l], writes=[r_ps])
            emit_copy_evac(b, ps, r_ps, 128, 512,
                           V_d[tok0 + tb * 128:tok0 + (tb + 1) * 128, vcol0 + nb * 512:vcol0 + (nb + 1) * 512])


def build_stage_diff_qkv():
    b = Builder()
    P = b.P
    xin = b.inp("xT", [D, 2304])
    modv = b.inp("modv", [128, 96, 2])
    n1g = b.inp("n1g", [128, 16])
    w_qkv = b.inp("w_qkv", [D, 6144])
    cos_d = b.inp("cosT", [128, 2048])
    sin_d = b.inp("sinT", [128, 2048])
    prot_d = b.inp("prot", [128, 128])
    QT = b.out("QT", [16, 128, 2304], BF16)
    KT = b.out("KT", [16, 128, 2304], BF16)
    V = b.out("V", [2304, 2048], BF16)
    b.setup_common()
    b.obf = b.rot_sb(3, [128, TT], BF16)
    A1, B1, G1 = emit_modprep(b, modv, n1g, 0, 1, 2)
    cosT, sinT, prot, r_cs = load_rope_tiles(b, cos_d, sin_d, prot_d, 128)
    hT = b.sb([128, 16, TT], BF16)
    r_h = Res("hT")
    for tile in tiles_for(True):
        tok0, segs = tile
        n = tile_n(tile)
        nl = n_latent(tile)
        emit_norm_mod(b, xin, tile, A1, B1, hT, r_h)
        emit_load_cs(b, cosT, sinT, r_cs, cos_d, sin_d, 128, tok0, nl)

        def evac(m, ps, r_ps):
            dst = (QT if m < 16 else KT)[m % 16, :, tok0:tok0 + n]
            if DEBUG_MODE == "norope":
                emit_copy_evac(b, ps, r_ps, 128, n, dst)
            else:
                emit_rope_evac(b, ps, r_ps, nl, n, 128, prot, cosT, sinT, r_cs, dst)
        emit_linear_fm(b, w_qkv, 0, KC, 0, 32, hT, r_h, tile, evac)
        if DEBUG_MODE != "nov":
            emit_v_tokmajor(b, w_qkv, 0, KC, 4096, 2048, hT, r_h, tile, V, 0)
    return b


KG = [(0, 22), (22, 22), (44, 22)]
KG_CTX = [(64, 2)]


def setup_attn_buffers(b):
    b.big = b.sb([128, 32 * TT], BF16)
    b.hT = b.sb([128, 16, TT], BF16)
    b.r_h = Res("hT")
    b.r_p0 = Res("piece0")
    b.r_p1 = Res("piece1")
    b.hid = b.big[:, :].rearrange("p (k t) -> p k t", k=32)
    b.r_hid = [b.r_p0, b.r_p1]
    pieces = [b.big[:, 0:12288], b.big[:, 12288:24576], b.hT[:, :, :].rearrange("p k t -> p (k t)")]
    b.pieces = Rot(pieces)
    b.pieces.res = [b.r_p0, b.r_p1, b.r_h]
    b.pt = b.rot_sb(3, [128, 512], BF16)
    b.ef = b.rot_sb(6, [128, 512], F32)
    b.obf = b.rot_sb(3, [128, TT], BF16)


def emit_diff_attn(b, h, q0, qn, groups, Qh, r_q, KT_all, V_all, aoT, r_ao_w, neglam, gs, r_sc, scale):
    P = b.P
    T = b.ps2.tiles
    bank = b.bank
    O = [[(T[j][:, eh * 512:eh * 512 + qn], bank[2 * j + eh]) for eh in range(2)] for j in range(2)]
    L = [(T[2][:, j * 512:j * 512 + qn], bank[4 + j]) for j in range(2)]
    Sc = [(T[3][:, i * 512:i * 512 + qn], bank[6 + i]) for i in range(2)]
    si = 0
    for gi, (kc0, nkc) in enumerate(groups):
        buf, r_pc = b.pieces.next()
        Kp = [buf[:, j * 2816:j * 2816 + nkc * 128] for j in range(2)]
        Vp = buf[:, 5632:5632 + nkc * 256].rearrange("p (c e) -> p c e", e=256)
        for j in range(2):
            P.add("sp", lambda e, j=j: e.dma_start(out=Kp[j], in_=KT_all[h * 2 + j, :, kc0 * 128:(kc0 + nkc) * 128]),
                  writes=[r_pc], dma=True)
        P.add("sp", lambda e: e.dma_start(
            out=Vp, in_=V_all[kc0 * 128:(kc0 + nkc) * 128, h * 256:(h + 1) * 256].rearrange("(c p) e -> p c e", p=128)),
            writes=[r_pc], dma=True)
        for j in range(2):
            for c in range(nkc):
                first = (gi == 0 and c == 0)
                last = (gi == len(groups) - 1 and c == nkc - 1)
                sc, r_s = Sc[si]
                si ^= 1
                P.add("pe", lambda e, j=j, c=c, sc=sc: e.matmul(sc, lhsT=Kp[j][:, c * 128:(c + 1) * 128],
                                                               rhs=Qh[:, j, q0:q0 + qn], start=True, stop=True),
                      reads=[r_pc, r_q], writes=[r_s])
                pt, r_pt = b.pt.next()
                P.add("act", lambda e, sc=sc, pt=pt: e.activation(out=pt[:, 0:qn], in_=sc, func=AF.Exp, scale=scale),
                      reads=[r_s], writes=[r_pt])
                for eh in range(2):
                    P.add("pe", lambda e, j=j, c=c, eh=eh, pt=pt: e.matmul(
                        O[j][eh][0], lhsT=Vp[:, c, eh * 128:(eh + 1) * 128], rhs=pt[:, 0:qn], start=first, stop=last),
                        reads=[r_pc, r_pt], writes=[O[j][eh][1]])
                P.add("pe", lambda e, j=j, pt=pt: e.matmul(L[j][0], lhsT=b.ones[:, :], rhs=pt[:, 0:qn], start=first, stop=last),
                      reads=[r_pt, b.r_ones], writes=[L[j][1]])
    rl = []
    for j in range(2):
        t, r_t = b.ef.next()
        P.add("dve", lambda e, j=j, t=t: e.reciprocal(out=t[:, 0:qn], in_=L[j][0]), reads=[L[j][1]], writes=[r_t])
        rl.append((t, r_t))
    oe = []
    for eh in range(2):
        t1, r_t1 = b.ef.next()
        P.add("dve", lambda e, eh=eh, t1=t1: e.tensor_tensor(out=t1[:, 0:qn], in0=O[0][eh][0], in1=rl[0][0][:, 0:qn], op=ALU.mult),
              reads=[O[0][eh][1], rl[0][1]], writes=[r_t1])
        t2, r_t2 = b.ef.next()
        P.add("dve", lambda e, eh=eh, t2=t2: e.tensor_tensor(out=t2[:, 0:qn], in0=O[1][eh][0], in1=rl[1][0][:, 0:qn], op=ALU.mult),
              reads=[O[1][eh][1], rl[1][1]], writes=[r_t2])
        P.add("dve", lambda e, t1=t1, t2=t2: e.scalar_tensor_tensor(
            out=t1[:, 0:qn], in0=t2[:, 0:qn], scalar=neglam[:, 0:1], in1=t1[:, 0:qn], op0=ALU.mult, op1=ALU.add),
            reads=[r_t1, r_t2, r_sc], writes=[r_t1])
        oe.append((t1, r_t1))
    ssb, r_ssb = Sc[0]
    for eh in range(2):
        sq, r_sq = b.pt.next()
        P.add("act", lambda e, eh=eh, sq=sq: e.activation(out=sq[:, 0:qn], in_=oe[eh][0][:, 0:qn], func=AF.Square),
              reads=[oe[eh][1]], writes=[r_sq])
        P.add("pe", lambda e, eh=eh, sq=sq: e.matmul(ssb, lhsT=b.ones[:, :], rhs=sq[:, 0:qn], start=(eh == 0), stop=(eh == 1)),
              reads=[r_sq, b.r_ones], writes=[r_ssb])
    rs, r_rs = rl[0]
    P.add("act", lambda e: e.activation(out=rs[:, 0:qn], in_=ssb, func=AF.Sqrt, scale=1.0 / 256, bias=b.epst[:, 0:1]),
          reads=[r_ssb, b.r_ones], writes=[r_rs])
    P.add("dve", lambda e: e.reciprocal(out=rs[:, 0:qn], in_=rs[:, 0:qn]), reads=[r_rs], writes=[r_rs])
    for eh in range(2):
        ob, r_ob = b.obf.next()
        P.add("dve", lambda e, eh=eh, ob=ob: e.scalar_tensor_tensor(
            out=ob[:, 0:qn], in0=oe[eh][0][:, 0:qn], scalar=gs[:, eh:eh + 1], in1=rs[:, 0:qn], op0=ALU.mult, op1=ALU.mult),
            reads=[oe[eh][1], r_rs, r_sc], writes=[r_ob])
        row = h * 256 + eh * 128
        P.add("sp", lambda e, ob=ob, row=row: e.dma_start(out=aoT[row:row + 128, q0:q0 + qn], in_=ob[:, 0:qn]),
              reads=[r_ob], writes=[r_ao_w[h * 2 + eh]], dma=True)


def emit_oproj_ffn(b, tiles, aoT, r_ao, w_o, xin, xmid, xo, w1, w2, A2, B2, G1, G2, qblocks, final=None):
    P = b.P
    for tile in tiles:
        tok0, segs = tile
        n = tile_n(tile)
        r_xmid = [Res() for _ in range(KC)]
        r_xo = [Res() for _ in range(KC)]
        for k in range(KC):
            deps = [r_ao[k][qi] for qi, (q0, qn) in enumerate(qblocks) if q0 < tok0 + n and q0 + qn > tok0]
            P.add("sp", lambda e, k=k: e.dma_start(out=b.hT[:, k, 0:n], in_=aoT[k * 128:(k + 1) * 128, tok0:tok0 + n]),
                  reads=deps, writes=[b.r_h], dma=True)

        def evac_o(m, ps, r_ps):
            emit_residual_evac(b, xin, xmid, r_xmid, tile, G1, m, ps, r_ps)
        emit_linear_fm(b, w_o, 0, KC, 0, 16, b.hT, b.r_h, tile, evac_o)
        emit_norm_mod_dep(b, xmid, r_xmid, tile, A2, B2, b.hT, b.r_h)
        emit_ffn(b, tile, b.hT, b.r_h, b.hid, b.r_hid, xmid, xo, r_xo, w1, w2, G2)


def build_stage_diff_attn(lambda_init):
    b = Builder()
    P = b.P
    xin = b.inp("xT", [D, 2304])
    modv = b.inp("modv", [128, 96, 2])
    n2g = b.inp("n2g", [128, 16])
    QT = b.inp("QT", [16, 128, 2304], BF16)
    KT_all = b.inp("KT_all", [16, 128, 8448], BF16)
    V_all = b.inp("V_all", [8448, 2048], BF16)
    lamv = b.inp("lamv", [128, 4, 128])
    sg_d = b.inp("subg", [128, 2])
    w_o = b.inp("w_o", [D, D])
    w1 = b.inp("w1", [D, 8192])
    w2 = b.inp("w2", [8192, D])
    xo = b.out("xo", [D, 2304])
    if DEBUG_MODE == "dbg":
        xmid = b.out("xmid", [D, 2304])
        aoT = b.out("aoT", [D, 2304], BF16)
    else:
        xmid = b.scratch("xmid", [D, 2304])
        aoT = b.scratch("aoT", [D, 2304], BF16)
    b.setup_common()
    setup_attn_buffers(b)
    A2, B2, G2 = emit_modprep(b, modv, n2g, 3, 4, 5)
    G1 = (b.last_modv[0][:, 2 * 16:3 * 16, :], b.last_modv[1])
    lam_t, r_lam = b.load_small(lamv, [128, 4, 128])
    sg, r_sg = b.load_small(sg_d, [128, 2])
    r_sc = Res("scal")
    prod = b.sb([128, 128], F32)
    ssum = b.sb([128, 2], F32)
    neglam = b.sb([128, 1], F32)
    gs = b.sb([128, 2], F32)
    for i in range(2):
        P.add("dve", lambda e, i=i: e.tensor_tensor(out=prod[:, :], in0=lam_t[:, 2 * i, :], in1=lam_t[:, 2 * i + 1, :], op=ALU.mult),
              reads=[r_lam], writes=[r_sc])
        P.add("act", lambda e, i=i: e.activation(out=prod[:, :], in_=prod[:, :], func=AF.Identity, accum_out=ssum[:, i:i + 1]),
              reads=[r_sc], writes=[r_sc])
    P.add("act", lambda e: e.activation(out=ssum[:, :], in_=ssum[:, :], func=AF.Exp), reads=[r_sc], writes=[r_sc])
    P.add("dve", lambda e: e.tensor_tensor(out=neglam[:, :], in0=ssum[:, 1:2], in1=ssum[:, 0:1], op=ALU.subtract),
          reads=[r_sc], writes=[r_sc])
    P.add("dve", lambda e: e.tensor_scalar(out=neglam[:, :], in0=neglam[:, :], scalar1=-float(lambda_init), scalar2=None, op0=ALU.add),
          reads=[r_sc], writes=[r_sc])
    P.add("dve", lambda e: e.tensor_scalar(out=gs[:, :], in0=sg[:, :], scalar1=float(1.0 - lambda_init), scalar2=None, op0=ALU.mult),
          reads=[r_sg, r_sc], writes=[r_sc])
    qblocks = [(0, 512), (512, 512), (1024, 512), (1536, 512), (2048, 256)]
    r_ao = [[Res() for _ in qblocks] for _ in range(KC)]
    Qrot = b.rot_sb(2, [128, 2, 2304], BF16)
    scale = 128 ** -0.5
    for h in range(8):
        Qh, r_q = Qrot.next()
        P.add("sp", lambda e, h=h, Qh=Qh: e.dma_start(out=Qh[:], in_=QT[2 * h:2 * h + 2].rearrange("j p t -> p j t")),
              writes=[r_q], dma=True)
        for qi, (q0, qn) in enumerate(qblocks):
            groups = KG if q0 < 2048 else KG_CTX
            r_w = [r_ao[k][qi] for k in range(KC)]
            emit_diff_attn(b, h, q0, qn, groups, Qh, r_q, KT_all, V_all, aoT, r_w, neglam, gs, r_sc, scale)
    emit_oproj_ffn(b, tiles_for(True), aoT, r_ao, w_o, xin, xmid, xo, w1, w2, A2, B2, G1, G2, qblocks)
    return b


def build_stage_mla_qkv():
    b = Builder()
    P = b.P
    xin = b.inp("xT", [D, 2304])
    modv = b.inp("modv", [128, 96, 2])
    n1g = b.inp("n1g", [128, 16])
    w_dqkv = b.inp("w_dqkv", [D, 1024])
    qg_d = b.inp("qg", [128, 4])
    kvg_d = b.inp("kvg", [128, 4])
    w_uq = b.inp("w_uq", [448, 3072])
    w_ukv = b.inp("w_ukv", [512, 4096])
    cos_d = b.inp("cosT", [64, 2048])
    sin_d = b.inp("sinT", [64, 2048])
    prot_d = b.inp("prot", [64, 64])
    QnT = b.out("QnT", [16, 128, 2048], BF16)
    QrT = b.out("QrT", [16, 64, 2048], BF16)
    KnT = b.out("KnT", [16, 128, 2304], BF16)
    KrT = b.out("KrT", [64, 2304], BF16)
    V = b.out("V", [2304, 2048], BF16)
    b.setup_common(n_w=1)
    b.obf = b.rot_sb(3, [128, TT], BF16)
    A1, B1, G1 = emit_modprep(b, modv, n1g, 0, 1, 2)
    cosT, sinT, prot, r_cs = load_rope_tiles(b, cos_d, sin_d, prot_d, 64)
    qg, r_qg = b.load_small(qg_d, [128, 4])
    kvg, r_kvg = b.load_small(kvg_d, [128, 4])
    r_wc = Res("wconst")
    wdq = b.sb([128, 16, 1024], BF16)
    for i in range(2):
        P.add("pool", lambda e, i=i: e.dma_start(out=wdq[:, :, i * 512:(i + 1) * 512], in_=wview(w_dqkv, 0, 16, i * 512, 512)),
              writes=[r_wc], dma=True)
    wuq = b.sb([128, 4, 3072], BF16)
    P.add("pool", lambda e: e.dma_start(out=wuq[:, 0:3, :], in_=wview(w_uq, 0, 3, 0, 3072)), writes=[r_wc], dma=True)
    P.add("pool", lambda e: e.dma_start(out=wuq[0:64, 3, :], in_=w_uq[384:448, :]), writes=[r_wc], dma=True)
    wukv = b.sb([128, 4, 4096], BF16)
    for i in range(2):
        P.add("pool", lambda e, i=i: e.dma_start(out=wukv[:, :, i * 2048:(i + 1) * 2048], in_=wview(w_ukv, 0, 4, i * 2048, 2048)),
              writes=[r_wc], dma=True)
    hT = b.sb([128, 16, TT], BF16)
    r_h = Res("hT")
    cf = b.sb([128, 4, TT], F32)
    r_cf = Res("cf")
    sqs = b.sb([128, 4, TT], BF16)
    r_sqs = Res("sqs")
    cqn = b.sb([128, 4, TT], BF16)
    r_cqn = Res("cqn")
    ckvn = b.sb([128, 4, TT], BF16)
    r_ckvn = Res("ckvn")
    Rn = b.sb([128, TT], F32)
    r_Rn = Res("Rn")
    KQ = [128, 128, 128, 64]
    for tile in tiles_for(True):
        tok0, segs = tile
        n = tile_n(tile)
        nl = n_latent(tile)
        segs_lat = [s for s in segs if s[2] == 0]
        emit_norm_mod(b, xin, tile, A1, B1, hT, r_h)
        emit_load_cs(b, cosT, sinT, r_cs, cos_d, sin_d, 64, tok0, nl)

        def latent_part(col0, Ms, gt, r_g, nfeat, dst, r_dst):
            for i, M in enumerate(Ms):
                ps, r_ps = b.ps2.next()
                for k in range(KC):
                    for (s0, sn, _) in segs:
                        P.add("pe", lambda e, k=k, s0=s0, sn=sn, M=M, i=i: e.matmul(
                            ps[0:M, s0:s0 + sn], lhsT=wdq[:, k, col0 + i * 128:col0 + i * 128 + M], rhs=hT[:, k, s0:s0 + sn],
                            start=(k == 0), stop=(k == KC - 1)), reads=[r_wc, r_h], writes=[r_ps])
                P.add("act", lambda e, M=M, i=i: e.activation(out=cf[0:M, i, 0:n], in_=ps[0:M, 0:n], func=AF.Identity),
                      reads=[r_ps], writes=[r_cf])
                P.add("act", lambda e, M=M, i=i: e.activation(out=sqs[0:M, i, 0:n], in_=cf[0:M, i, 0:n], func=AF.Square),
                      reads=[r_cf], writes=[r_sqs])
            ps, r_ps = b.ps2.next()
            for i, M in enumerate(Ms):
                for (s0, sn, _) in segs:
                    P.add("pe", lambda e, s0=s0, sn=sn, M=M, i=i: e.matmul(
                        ps[:, s0:s0 + sn], lhsT=b.ones[0:M, :], rhs=sqs[0:M, i, s0:s0 + sn],
                        start=(i == 0), stop=(i == len(Ms) - 1)), reads=[r_sqs, b.r_ones], writes=[r_ps])
            P.add("act", lambda e: e.activation(out=Rn[:, 0:n], in_=ps[:, 0:n], func=AF.Sqrt, scale=1.0 / nfeat, bias=b.epst[:, 0:1]),
                  reads=[r_ps, b.r_ones], writes=[r_Rn])
            P.add("dve", lambda e: e.reciprocal(out=Rn[:, 0:n], in_=Rn[:, 0:n]), reads=[r_Rn], writes=[r_Rn])
            for i, M in enumerate(Ms):
                P.add("dve", lambda e, M=M, i=i: e.scalar_tensor_tensor(
                    out=dst[0:M, i, 0:n], in0=cf[0:M, i, 0:n], scalar=gt[0:M, i:i + 1], in1=Rn[0:M, 0:n],
                    op0=ALU.mult, op1=ALU.mult), reads=[r_cf, r_g, r_Rn], writes=[r_dst])
        latent_part(0, KQ, qg, r_qg, 448.0, cqn, r_cqn)
        latent_part(448, [128] * 4, kvg, r_kvg, 512.0, ckvn, r_ckvn)
        ps, r_ps = b.ps2.next()
        for k in range(KC):
            for (s0, sn, _) in segs:
                P.add("pe", lambda e, k=k, s0=s0, sn=sn: e.matmul(
                    ps[0:64, s0:s0 + sn], lhsT=wdq[:, k, 960:1024], rhs=hT[:, k, s0:s0 + sn],
                    start=(k == 0), stop=(k == KC - 1)), reads=[r_wc, r_h], writes=[r_ps])
        emit_rope_evac(b, ps, r_ps, nl, n, 64, prot, cosT, sinT, r_cs, KrT[:, tok0:tok0 + n])
        for h in range(16):
            for part in range(2):
                c0 = h * 192 + part * 128
                M = 128 if part == 0 else 64
                ps, r_ps = b.ps2.next()
                for k in range(4):
                    for (s0, sn, _) in segs_lat:
                        P.add("pe", lambda e, k=k, s0=s0, sn=sn, c0=c0, M=M: e.matmul(
                            ps[0:M, s0:s0 + sn], lhsT=wuq[0:KQ[k], k, c0:c0 + M], rhs=cqn[0:KQ[k], k, s0:s0 + sn],
                            start=(k == 0), stop=(k == 3)), reads=[r_wc, r_cqn], writes=[r_ps])
                if part == 0:
                    emit_copy_evac(b, ps, r_ps, 128, nl, QnT[h, :, tok0:tok0 + nl])
                else:
                    emit_rope_evac(b, ps, r_ps, nl, nl, 64, prot, cosT, sinT, r_cs, QrT[h, :, tok0:tok0 + nl])
            ps, r_ps = b.ps2.next()
            for k in range(4):
                for (s0, sn, _) in segs:
                    P.add("pe", lambda e, k=k, s0=s0, sn=sn, h=h: e.matmul(
                        ps[:, s0:s0 + sn], lhsT=wukv[:, k, h * 256:h * 256 + 128], rhs=ckvn[:, k, s0:s0 + sn],
                        start=(k == 0), stop=(k == 3)), reads=[r_wc, r_ckvn], writes=[r_ps])
            emit_copy_evac(b, ps, r_ps, 128, n, KnT[h, :, tok0:tok0 + n])
        for tb in range(n // 128):
            for hb in range(4):
                ps, r_ps = b.ps2.next()
                for k in range(4):
                    rhs = wukv[:, k, :].rearrange("p (h t c) -> p h t c", h=16, t=2)[:, 4 * hb:4 * hb + 4, 1, :]
                    P.add("pe", lambda e, k=k, tb=tb, rhs=rhs: e.matmul(
                        ps[:, 0:512].rearrange("p (h c) -> p h c", h=4), lhsT=ckvn[:, k, tb * 128:(tb + 1) * 128], rhs=rhs,
                        start=(k == 0), stop=(k == 3)), reads=[r_wc, r_ckvn], writes=[r_ps])
                emit_copy_evac(b, ps, r_ps, 128, 512, V[tok0 + tb * 128:tok0 + (tb + 1) * 128, hb * 512:(hb + 1) * 512])
    return b


def emit_mla_attn(b, h, q0, qn, groups, Qh, r_q, krs, r_kr, KnT_all, V_all, aoT, r_ao_w, scale, par):
    P = b.P
    T = b.ps2.tiles
    bank = b.bank
    O = (T[par][:, 0:qn], bank[2 * par])
    L = (T[par][:, 512:512 + qn], bank[2 * par + 1])
    Sc = [(T[3][:, i * 512:i * 512 + qn], bank[6 + i]) for i in range(2)]
    chunks = []
    for gi, (kc0, nkc) in enumerate(groups):
        buf, r_pc = b.pieces.next()
        Kp = buf[:, 0:nkc * 128]
        Vp = buf[:, 2816:2816 + nkc * 128].rearrange("p (c e) -> p c e", e=128)
        P.add("sp", lambda e: e.dma_start(out=Kp, in_=KnT_all[h, :, kc0 * 128:(kc0 + nkc) * 128]), writes=[r_pc], dma=True)
        P.add("sp", lambda e: e.dma_start(
            out=Vp, in_=V_all[kc0 * 128:(kc0 + nkc) * 128, h * 128:(h + 1) * 128].rearrange("(c p) e -> p c e", p=128)),
            writes=[r_pc], dma=True)
        for c in range(nkc):
            chunks.append((Kp, Vp, r_pc, c, kc0 + c))
    nch = len(chunks)

    def score(i):
        Kp, Vp, r_pc, c, gc = chunks[i]
        sc, r_s = Sc[i % 2]
        P.add("pe", lambda e: e.matmul(sc, lhsT=Kp[:, c * 128:(c + 1) * 128], rhs=Qh[:, 0, q0:q0 + qn], start=True, stop=False),
              reads=[r_pc, r_q], writes=[r_s])
        P.add("pe", lambda e: e.matmul(sc, lhsT=krs[0:64, gc * 128:(gc + 1) * 128], rhs=Qh[0:64, 1, q0:q0 + qn], start=False, stop=True),
              reads=[r_kr, r_q], writes=[r_s])
    score(0)
    for i in range(nch):
        if i + 1 < nch:
            score(i + 1)
        Kp, Vp, r_pc, c, gc = chunks[i]
        sc, r_s = Sc[i % 2]
        pt, r_pt = b.pt.next()
        P.add("act", lambda e: e.activation(out=pt[:, 0:qn], in_=sc, func=AF.Exp, scale=scale), reads=[r_s], writes=[r_pt])
        P.add("pe", lambda e: e.matmul(O[0], lhsT=Vp[:, c, :], rhs=pt[:, 0:qn], start=(i == 0), stop=(i == nch - 1)),
              reads=[r_pc, r_pt], writes=[O[1]])
        P.add("pe", lambda e: e.matmul(L[0], lhsT=b.ones[:, :], rhs=pt[:, 0:qn], start=(i == 0), stop=(i == nch - 1)),
              reads=[r_pt, b.r_ones], writes=[L[1]])
    rl, r_rl = b.ef.next()
    P.add("dve", lambda e: e.reciprocal(out=rl[:, 0:qn], in_=L[0]), reads=[L[1]], writes=[r_rl])
    ob, r_ob = b.obf.next()
    P.add("dve", lambda e: e.tensor_tensor(out=ob[:, 0:qn], in0=O[0], in1=rl[:, 0:qn], op=ALU.mult),
          reads=[O[1], r_rl], writes=[r_ob])
    P.add("sp", lambda e: e.dma_start(out=aoT[h * 128:(h + 1) * 128, q0:q0 + qn], in_=ob[:, 0:qn]),
          reads=[r_ob], writes=[r_ao_w[h]], dma=True)


def build_stage_mla_attn():
    b = Builder()
    P = b.P
    xin = b.inp("xT", [D, 2048])
    modv = b.inp("modv", [128, 96, 2])
    n2g = b.inp("n2g", [128, 16])
    QnT = b.inp("QnT", [16, 128, 2048], BF16)
    QrT = b.inp("QrT", [16, 64, 2048], BF16)
    KnT_all = b.inp("KnT_all", [16, 128, 8448], BF16)
    KrT_all = b.inp("KrT_all", [64, 8448], BF16)
    V_all = b.inp("V_all", [8448, 2048], BF16)
    w_o = b.inp("w_o", [D, D])
    w1 = b.inp("w1", [D, 8192])
    w2 = b.inp("w2", [8192, D])
    xo = b.out("xo", [D, 2048])
    xmid = b.scratch("xmid", [D, 2048])
    aoT = b.scratch("aoT", [D, 2048], BF16)
    b.setup_common(n_w=3)
    setup_attn_buffers(b)
    A2, B2, G2 = emit_modprep(b, modv, n2g, 3, 4, 5)
    G1 = (b.last_modv[0][:, 2 * 16:3 * 16, :], b.last_modv[1])
    krs = b.sb([64, 8448], BF16)
    r_kr = Res("kr")
    P.add("sp", lambda e: e.dma_start(out=krs[:, :], in_=KrT_all), writes=[r_kr], dma=True)
    qblocks = [(0, 512), (512, 512), (1024, 512), (1536, 512)]
    r_ao = [[Res() for _ in qblocks] for _ in range(KC)]
    Qrot = b.rot_sb(2, [128, 2, 2048], BF16)
    scale = 192 ** -0.5
    par = 0
    for h in range(16):
        Qh, r_q = Qrot.next()
        P.add("sp", lambda e: e.dma_start(out=Qh[:, 0, :], in_=QnT[h]), writes=[r_q], dma=True)
        P.add("sp", lambda e: e.dma_start(out=Qh[0:64, 1, :], in_=QrT[h]), writes=[r_q], dma=True)
        for qi, (q0, qn) in enumerate(qblocks):
            r_w = [r_ao[k][qi] for k in range(KC)]
            emit_mla_attn(b, h, q0, qn, KG, Qh, r_q, krs, r_kr, KnT_all, V_all, aoT, r_w, scale, par)
            par ^= 1
    emit_oproj_ffn(b, tiles_for(False), aoT, r_ao, w_o, xin, xmid, xo, w1, w2, A2, B2, G1, G2, qblocks)
    return b


_PROGS = {}


def _prog(name, fn):
    if name not in _PROGS:
        _PROGS[name] = fn().finish()
    return _PROGS[name]


def _run(nc, in_maps):
    res = run_bass_kernel_spmd(nc, in_maps, core_ids=list(range(8)))
    return res.results


def _vecT(v):
    return np.ascontiguousarray(np.asarray(v, np.float32).reshape(-1, 128).T)


def _make_prot(DR):
    hf = DR // 4
    p = np.zeros((DR, DR), np.float32)
    for m in range(DR):
        if (m // hf) % 2 == 0:
            p[m + hf, m] = -1.0
        else:
            p[m - hf, m] = 1.0
    return p


def _rope_tables(rot_dim):
    a = rot_dim // 2
    inv = np.power(np.float32(10000.0), -np.arange(0, a, 2, dtype=np.float32) / np.float32(a)).astype(np.float32)
    rows = np.repeat(np.arange(SEQ // 64, dtype=np.int32), 64)
    cols = np.tile(np.arange(64, dtype=np.int32), SEQ // 64)

    def ax(pos):
        ang = pos.astype(np.float32)[:, None] * inv[None, :]
        return np.concatenate([ang, ang], axis=-1)
    ang = np.concatenate([ax(rows), ax(cols)], axis=-1).astype(np.float32)
    return np.cos(ang).astype(np.float32), np.sin(ang).astype(np.float32)


def kernel(x, c, ctx, c_ctx, w_mod, b_mod, norm1_g, norm2_g, w_ff1, w_ff2,
           a_w_in, a_b_in, a_ln_g, a_w_s, a_b_s, a_w_out,
           b_w_qkv, b_lam_q1, b_lam_k1, b_lam_q2, b_lam_k2, b_subln_g, b_w_o,
           c_w_dqkv, c_q_norm_g, c_w_uq, c_kv_norm_g, c_w_ukv, c_w_o, final_g):
    f32 = np.float32
    x = np.asarray(x, f32)
    ctx = np.asarray(ctx, f32)
    w_mod = np.asarray(w_mod, f32)
    b_mod = np.asarray(b_mod, f32)
    cores = list(range(8))
    c_all = np.concatenate([np.asarray(c, f32), np.asarray(c_ctx, f32)[None]], 0)
    cT = np.ascontiguousarray(c_all.reshape(3, 16, 128).transpose(2, 1, 0))
    ins = []
    for i in cores:
        bm = b_mod[:, i * 1536:(i + 1) * 1536].reshape(4, 12, 128).transpose(2, 0, 1).reshape(128, 48)
        ins.append({"wm": np.ascontiguousarray(w_mod[:, :, i * 1536:(i + 1) * 1536]),
                    "bm": np.ascontiguousarray(bm), "cT": cT})
    rm = _run(_prog("mod", build_stage_mod), ins)
    mod_full = np.zeros((4, 3, 12288), f32)
    for i in cores:
        mo = rm[i]["modo"].reshape(128, 4, 12, 3)
        mod_full[:, :, i * 1536:(i + 1) * 1536] = mo.transpose(1, 3, 2, 0).reshape(4, 3, 1536)

    def modv_for(l, b):
        rows = [mod_full[l, row].reshape(6, 16, 128).transpose(2, 0, 1).reshape(128, 96) for row in (b, 2)]
        return np.ascontiguousarray(np.stack(rows, -1))

    def gmlp_consts(idx):
        return {"w_in": np.asarray(a_w_in[idx], f32), "b_u": _vecT(a_b_in[idx][:2048]),
                "b_v": np.ascontiguousarray(np.broadcast_to(np.asarray(a_b_in[idx][2048:], f32), (128, 2048))),
                "ln_g": _vecT(a_ln_g[idx]),
                "wsT": np.ascontiguousarray(np.asarray(a_w_s[idx], f32).transpose(2, 0, 1)),
                "b_s": np.ascontiguousarray(np.broadcast_to(np.asarray(a_b_s[idx], f32)[None], (128, 16, 128))),
                "w_out": np.asarray(a_w_out[idx], f32)}

    L = 0
    ins = []
    g0 = gmlp_consts(0)
    for r in cores:
        b, q = r // 4, r % 4
        xT = np.concatenate([x[b, q * 2048:(q + 1) * 2048].T, ctx[b].T], 1)
        m = {"xT": np.ascontiguousarray(xT), "modv": modv_for(L, b), "n1g": _vecT(norm1_g[L]), "n2g": _vecT(norm2_g[L]),
             "w1": np.asarray(w_ff1[L], f32), "w2": np.asarray(w_ff2[L], f32)}
        m.update(g0)
        ins.append(m)
    r0 = _run(_prog("gmlp_ctx", lambda: build_stage_gmlp_layer(True, False)), ins)
    xT_cur = [r0[r]["xo"] for r in cores]

    L = 1
    cos_b, sin_b = _rope_tables(128)
    prot128 = _make_prot(128)
    ins = []
    for r in cores:
        b, q = r // 4, r % 4
        ins.append({"xT": xT_cur[r], "modv": modv_for(L, b), "n1g": _vecT(norm1_g[L]), "w_qkv": np.asarray(b_w_qkv[0], f32),
                    "cosT": np.ascontiguousarray(cos_b[q * 2048:(q + 1) * 2048].T),
                    "sinT": np.ascontiguousarray(sin_b[q * 2048:(q + 1) * 2048].T), "prot": prot128})
    ra = _run(_prog("diff_qkv", build_stage_diff_qkv), ins)
    lam_init = 0.8 - 0.6 * math.exp(-0.3 * L)
    lamv = np.stack([np.asarray(v[0], f32) for v in (b_lam_q1, b_lam_k1, b_lam_q2, b_lam_k2)], 0)
    lamv = np.ascontiguousarray(np.broadcast_to(lamv[None], (128, 4, 128)))
    ins = []
    for r in cores:
        b, q = r // 4, r % 4
        cs = [b * 4 + i for i in range(4)]
        KT_all = np.concatenate([ra[cc]["KT"][:, :, :2048] for cc in cs] + [ra[cs[0]]["KT"][:, :, 2048:]], 2)
        V_all = np.concatenate([ra[cc]["V"][:2048] for cc in cs] + [ra[cs[0]]["V"][2048:]], 0)
        ins.append({"xT": xT_cur[r], "modv": modv_for(L, b), "n2g": _vecT(norm2_g[L]), "QT": ra[r]["QT"],
                    "KT_all": np.ascontiguousarray(KT_all), "V_all": np.ascontiguousarray(V_all), "lamv": lamv,
                    "subg": _vecT(b_subln_g[0]), "w_o": np.asarray(b_w_o[0], f32),
                    "w1": np.asarray(w_ff1[L], f32), "w2": np.asarray(w_ff2[L], f32)})
    rb = _run(_prog("diff_attn", lambda: build_stage_diff_attn(lam_init)), ins)
    xT_cur = [rb[r]["xo"] for r in cores]

    L = 2
    cos_c, sin_c = _rope_tables(64)
    prot64 = _make_prot(64)
    qg = np.zeros(512, f32)
    qg[:448] = np.asarray(c_q_norm_g[0], f32)
    ins = []
    for r in cores:
        b, q = r // 4, r % 4
        ins.append({"xT": xT_cur[r], "modv": modv_for(L, b), "n1g": _vecT(norm1_g[L]),
                    "w_dqkv": np.asarray(c_w_dqkv[0], f32), "qg": _vecT(qg), "kvg": _vecT(c_kv_norm_g[0]),
                    "w_uq": np.asarray(c_w_uq[0], f32), "w_ukv": np.asarray(c_w_ukv[0], f32),
                    "cosT": np.ascontiguousarray(cos_c[q * 2048:(q + 1) * 2048].T),
                    "sinT": np.ascontiguousarray(sin_c[q * 2048:(q + 1) * 2048].T), "prot": prot64})
    ra = _run(_prog("mla_qkv", build_stage_mla_qkv), ins)
    ins = []
    for r in cores:
        b, q = r // 4, r % 4
        cs = [b * 4 + i for i in range(4)]
        Kn = np.concatenate([ra[cc]["KnT"][:, :, :2048] for cc in cs] + [ra[cs[0]]["KnT"][:, :, 2048:]], 2)
        Kr = np.concatenate([ra[cc]["KrT"][:, :2048] for cc in cs] + [ra[cs[0]]["KrT"][:, 2048:]], 1)
        Va = np.concatenate([ra[cc]["V"][:2048] for cc in cs] + [ra[cs[0]]["V"][2048:]], 0)
        ins.append({"xT": np.ascontiguousarray(xT_cur[r][:, :2048]), "modv": modv_for(L, b), "n2g": _vecT(norm2_g[L]),
                    "QnT": ra[r]["QnT"], "QrT": ra[r]["QrT"], "KnT_all": np.ascontiguousarray(Kn),
                    "KrT_all": np.ascontiguousarray(Kr), "V_all": np.ascontiguousarray(Va),
                    "w_o": np.asarray(c_w_o[0], f32), "w1": np.asarray(w_ff1[L], f32), "w2": np.asarray(w_ff2[L], f32)})
    rb = _run(_prog("mla_attn", build_stage_mla_attn), ins)
    xT_cur = [rb[r]["xo"] for r in cores]

    L = 3
    g1 = gmlp_consts(1)
    ins = []
    for r in cores:
        b, q = r // 4, r % 4
        m = {"xT": xT_cur[r], "modv": modv_for(L, b), "n1g": _vecT(norm1_g[L]), "n2g": _vecT(norm2_g[L]),
             "w1": np.asarray(w_ff1[L], f32), "w2": np.asarray(w_ff2[L], f32), "fg": _vecT(final_g)}
        m.update(g1)
        ins.append(m)
    r3 = _run(_prog("gmlp_final", lambda: build_stage_gmlp_layer(False, True)), ins)
    out = np.empty((2, SEQ, D), f32)
    for r in cores:
        b, q = r // 4, r % 4
        out[b, q * 2048:(q + 1) * 2048, :] = r3[r]["xo"].T
    return out
```

```python
import math
from contextlib import ExitStack
import numpy as np
import ml_dtypes
import concourse.bass as bass
import concourse.mybir as mybir
from concourse.bass_utils import run_bass_kernel_spmd

F32 = mybir.dt.float32
BF16 = mybir.dt.bfloat16
AF = mybir.ActivationFunctionType
ALU = mybir.AluOpType
NPBF16 = ml_dtypes.bfloat16

D = 2048
KC = 16
SEQ = 8192
CTX = 256
TT = 768
EPS = 1e-6
N_DMA_SEMS = 16
DEBUG_MODE = ""
STREAMS = ("pe", "act", "dve", "pool", "sp")


class Res:
    __slots__ = ("name", "writers", "readers", "dma_readers", "excl")

    def __init__(self, name="", excl=False):
        self.name = name
        self.excl = excl
        self.writers = []
        self.readers = {}
        self.dma_readers = []


class Op:
    __slots__ = ("st", "fn", "deps", "signal", "tick", "is_dma", "dsem", "dval", "prewait")

    def __init__(self, st, fn, is_dma):
        self.st = st
        self.fn = fn
        self.is_dma = is_dma
        self.deps = []
        self.signal = False
        self.tick = 0
        self.dsem = None
        self.dval = 0
        self.prewait = None


class _Rec:
    def __init__(self):
        self.call = None

    def __getattr__(self, name):
        def f(*a, **k):
            self.call = (name, a, k)
            return self
        return f


def _flat(rs):
    out = []
    for r in rs:
        if isinstance(r, (list, tuple)):
            out.extend(_flat(r))
        else:
            out.append(r)
    return out


class Prog:
    def __init__(self, nc):
        self.nc = nc
        self.ops = {s: [] for s in STREAMS}
        self.n_dma = {s: 0 for s in STREAMS}
        self.dma_streams = set()

    def add(self, st, fn, reads=(), writes=(), dma=False):
        rec = _Rec()
        fn(rec)
        op = Op(st, rec.call, dma)
        reads = _flat(reads)
        writes = _flat(writes)
        deps = {}
        for r in reads:
            for w in r.writers:
                deps[id(w)] = (w, True)
            if r.excl:
                for ost, o in r.readers.items():
                    if ost != st and id(o) not in deps:
                        deps[id(o)] = (o, False)
        join = {}
        for r in writes:
            has_readers = bool(r.readers) or bool(r.dma_readers)
            j = dma and (not has_readers) and len(r.writers) > 0 and all(w.is_dma for w in r.writers)
            join[id(r)] = j
            if not j:
                for w in r.writers:
                    if id(w) not in deps:
                        deps[id(w)] = (w, False)
            for o in r.readers.values():
                if id(o) not in deps:
                    deps[id(o)] = (o, False)
            for o in r.dma_readers:
                if id(o) not in deps:
                    deps[id(o)] = (o, False)
        for d, raw in deps.values():
            if d is op:
                continue
            if (not d.is_dma) and (not dma) and d.st == st:
                if st == "pe" or not raw:
                    continue
            if not d.is_dma:
                d.signal = True
            op.deps.append(d)
        for r in reads:
            if dma:
                r.dma_readers.append(op)
            else:
                r.readers[st] = op
        for r in writes:
            if join[id(r)]:
                r.writers.append(op)
            else:
                r.writers = [op]
                r.readers = {}
                r.dma_readers = []
        if dma:
            self.dma_streams.add(st)
            k = self.n_dma[st]
            self.n_dma[st] = k + 1
            op.dsem = k % N_DMA_SEMS
            op.dval = 16 * (k // N_DMA_SEMS + 1)
            if k >= N_DMA_SEMS:
                op.prewait = (op.dsem, 16 * (k // N_DMA_SEMS))
        self.ops[st].append(op)
        return op

    def emit(self):
        nc = self.nc
        with ExitStack() as es:
            csem = {}
            for s in ("pe", "act", "dve", "pool"):
                csem[s] = es.enter_context(nc.semaphore("c_" + s))
            dsem = {}
            for s in sorted(self.dma_streams):
                dsem[s] = [es.enter_context(nc.semaphore("d_%s_%d" % (s, i)))
                           for i in range(min(N_DMA_SEMS, self.n_dma[s]))]
            for s in STREAMS:
                t = 0
                for op in self.ops[s]:
                    if (not op.is_dma) and op.signal:
                        t += 1
                        op.tick = t
            block = es.enter_context(nc.Block())
            prog = self

            def run_stream(s, eng):
                waited = {}
                for op in prog.ops[s]:
                    need = {}
                    for d in op.deps:
                        if d.is_dma:
                            key = ("d", d.st, d.dsem)
                            sem = dsem[d.st][d.dsem]
                            val = d.dval
                        else:
                            key = ("c", d.st)
                            sem = csem[d.st]
                            val = d.tick
                        if need.get(key, (None, -1))[1] < val:
                            need[key] = (sem, val)
                    if op.prewait is not None:
                        key = ("d", s, op.prewait[0])
                        if need.get(key, (None, -1))[1] < op.prewait[1]:
                            need[key] = (dsem[s][op.prewait[0]], op.prewait[1])
                    for key, (sem, val) in need.items():
                        if waited.get(key, -1) >= val:
                            continue
                        waited[key] = val
                        eng.wait_ge(sem, val)
                    name_, a_, k_ = op.fn
                    inst = getattr(eng, name_)(*a_, **k_)
                    if op.is_dma:
                        inst.then_inc(dsem[s][op.dsem], 16)
                    elif op.signal:
                        inst.then_inc(csem[s], 1)
                if s == "sp":
                    for ds in sorted(prog.dma_streams):
                        n = prog.n_dma[ds]
                        for i in range(min(N_DMA_SEMS, n)):
                            cnt = (n - i + N_DMA_SEMS - 1) // N_DMA_SEMS
                            eng.wait_ge(dsem[ds][i], 16 * cnt)

            @block.tensor
            def _(e):
                run_stream("pe", e)

            @block.scalar
            def _(e):
                run_stream("act", e)

            @block.vector
            def _(e):
                run_stream("dve", e)

            @block.gpsimd
            def _(e):
                run_stream("pool", e)

            @block.sync
            def _(e):
                run_stream("sp", e)


class Rot:
    def __init__(self, tiles):
        self.tiles = tiles
        self.res = [Res() for _ in tiles]
        self.i = 0

    def next(self):
        j = self.i % len(self.tiles)
        self.i += 1
        return self.tiles[j], self.res[j]


class Builder:
    def __init__(self):
        self.nc = bass.Bass("TRN2", target_bir_lowering=False)
        self.P = Prog(self.nc)
        self.es = ExitStack()
        self.es.__enter__()
        self.in_names = []
        self.out_names = []
        self._n = 0

    def inp(self, name, shape, dt=F32):
        self.in_names.append(name)
        return self.nc.dram_tensor(name, list(shape), dt, kind="ExternalInput").ap()

    def out(self, name, shape, dt=F32):
        self.out_names.append(name)
        return self.nc.dram_tensor(name, list(shape), dt, kind="ExternalOutput").ap()

    def scratch(self, name, shape, dt=F32):
        return self.nc.dram_tensor(name, list(shape), dt, kind="Internal").ap()

    def sb(self, shape, dt, name=None):
        self._n += 1
        return self.es.enter_context(self.nc.sbuf_tensor(name or ("sb%d" % self._n), list(shape), dt))

    def psum(self, shape, dt=F32, name=None):
        self._n += 1
        return self.es.enter_context(self.nc.psum_tensor(name or ("ps%d" % self._n), list(shape), dt))

    def rot_sb(self, n, shape, dt):
        return Rot([self.sb(shape, dt) for _ in range(n)])

    def rot_ps(self, n, shape):
        return Rot([self.psum(shape) for _ in range(n)])

    def finish(self):
        self.P.emit()
        self.es.__exit__(None, None, None)
        return self.nc

    def setup_common(self, n_w=4):
        P = self.P
        self.wslots = self.rot_sb(n_w, [128, 8192], BF16)
        self.ps2 = self.rot_ps(4, [128, 1024])
        self.bank = [Res("bank%d" % i, excl=True) for i in range(8)]
        self.ps2.res = [[self.bank[2 * i], self.bank[2 * i + 1]] for i in range(4)]
        self.ones = self.sb([128, 128], BF16)
        self.r_ones = Res("ones")
        P.add("dve", lambda e: e.memset(self.ones[:], 1.0), writes=[self.r_ones])
        self.epst = self.sb([128, 1], F32)
        P.add("dve", lambda e: e.memset(self.epst[:], EPS), writes=[self.r_ones])
        self.xk = self.rot_sb(3, [128, TT], F32)
        self.tmpf = self.rot_sb(3, [128, TT], F32)
        self.sqb = self.rot_sb(2, [128, TT], BF16)
        self.Rt = self.sb([128, TT], F32)
        self.r_R = Res("R")

    def wload(self, src, nk, ncols):
        t, r = self.wslots.next()
        view = t[:, 0:nk * ncols].rearrange("p (k c) -> p k c", k=nk)
        self.P.add("pool", lambda e: e.dma_start(out=view, in_=src), writes=[r], dma=True)
        return view, r

    def load_small(self, src, shape, dt=F32, cast=False):
        t = self.sb(shape, dt)
        r = Res()
        self.P.add("pool" if cast else "sp", lambda e: e.dma_start(out=t[:], in_=src), writes=[r], dma=True)
        return t, r


def wview(w, k0, nk, c0, ncols):
    return w[k0 * 128:(k0 + nk) * 128, c0:c0 + ncols].rearrange("(k p) c -> p k c", p=128)


def _aslist(r):
    return list(r) if isinstance(r, (list, tuple)) else [r]


def tiles_for(with_ctx):
    t = [(0, [(0, 512, 0), (512, 256, 0)]), (768, [(0, 512, 0), (512, 256, 0)])]
    if with_ctx:
        t.append((1536, [(0, 512, 0), (512, 256, 1)]))
    else:
        t.append((1536, [(0, 512, 0)]))
    return t


def tile_n(tile):
    return sum(s[1] for s in tile[1])


def emit_modprep(b, modv_d, ng_d, j_shift, j_scale, j_gate):
    P = b.P
    modv, r_mod = b.load_small(modv_d, [128, 96, 2])
    ng, r_ng = b.load_small(ng_d, [128, 16])
    A = b.sb([128, 16, 2], F32)
    r_A = Res("A")
    for w in range(2):
        P.add("dve", lambda e, w=w: e.scalar_tensor_tensor(
            out=A[:, :, w], in0=modv[:, j_scale * 16:(j_scale + 1) * 16, w], scalar=1.0, in1=ng[:, :],
            op0=ALU.add, op1=ALU.mult), reads=[r_mod, r_ng], writes=[r_A])
    Bv = modv[:, j_shift * 16:(j_shift + 1) * 16, :]
    G = modv[:, j_gate * 16:(j_gate + 1) * 16, :]
    b.last_modv = (modv, r_mod)
    return (A, r_A), (Bv, r_mod), (G, r_mod)


def emit_rms_stats(b, x_d, tile, nfeat_chunks=KC, inv_n=1.0 / D):
    P = b.P
    tok0, segs = tile
    n = tile_n(tile)
    ps, r_ps = b.ps2.next()
    for k in range(nfeat_chunks):
        xk, r_xk = b.xk.next()
        P.add("sp", lambda e, xk=xk, k=k: e.dma_start(out=xk[:, 0:n], in_=x_d[k * 128:(k + 1) * 128, tok0:tok0 + n]),
              writes=[r_xk], dma=True)
        sq, r_sq = b.sqb.next()
        P.add("act", lambda e, xk=xk, sq=sq: e.activation(out=sq[:, 0:n], in_=xk[:, 0:n], func=AF.Square),
              reads=[r_xk], writes=[r_sq])
        for (c0, cn, _) in segs:
            P.add("pe", lambda e, sq=sq, c0=c0, cn=cn, k=k, ps=ps: e.matmul(
                ps[:, c0:c0 + cn], lhsT=b.ones[:, :], rhs=sq[:, c0:c0 + cn],
                start=(k == 0), stop=(k == nfeat_chunks - 1)),
                reads=[r_sq, b.r_ones], writes=[r_ps])
    R = b.Rt
    P.add("act", lambda e, ps=ps: e.activation(out=R[:, 0:n], in_=ps[:, 0:n], func=AF.Sqrt, scale=inv_n,
                                               bias=b.epst[:, 0:1]), reads=[r_ps, b.r_ones], writes=[b.r_R])
    P.add("dve", lambda e: e.reciprocal(out=R[:, 0:n], in_=R[:, 0:n]), reads=[b.r_R], writes=[b.r_R])


def emit_norm_mod(b, x_d, tile, A, Bv, hT, r_h):
    P = b.P
    tok0, segs = tile
    n = tile_n(tile)
    emit_rms_stats(b, x_d, tile)
    (At, r_A), (Bt, r_B) = A, Bv
    for k in range(KC):
        xk, r_xk = b.xk.next()
        P.add("sp", lambda e, xk=xk, k=k: e.dma_start(out=xk[:, 0:n], in_=x_d[k * 128:(k + 1) * 128, tok0:tok0 + n]),
              writes=[r_xk], dma=True)
        tm, r_tm = b.tmpf.next()
        P.add("dve", lambda e, xk=xk, tm=tm: e.tensor_tensor(out=tm[:, 0:n], in0=xk[:, 0:n], in1=b.Rt[:, 0:n],
                                                             op=ALU.mult), reads=[r_xk, b.r_R], writes=[r_tm])
        for (c0, cn, w) in segs:
            P.add("act", lambda e, tm=tm, c0=c0, cn=cn, w=w, k=k: e.activation(
                out=hT[:, k, c0:c0 + cn], in_=tm[:, c0:c0 + cn], func=AF.Identity,
                scale=At[:, k, w:w + 1], bias=Bt[:, k, w:w + 1]),
                reads=[r_tm, r_A, r_B], writes=[r_h])


def emit_linear_fm(b, w_d, k0, nk, c0, nchunks, rhsT, r_rhs, tile, evac, blk_chunks=4):
    P = b.P
    tok0, segs = tile
    ncols_blk = blk_chunks * 128
    assert nk * ncols_blk <= 8192
    nblk = (nchunks + blk_chunks - 1) // blk_chunks
    for blk in range(nblk):
        nch = min(blk_chunks, nchunks - blk * blk_chunks)
        wv, r_w = b.wload(wview(w_d, k0, nk, c0 + blk * ncols_blk, nch * 128), nk, nch * 128)
        for j in range(nch):
            m = blk * blk_chunks + j
            ps, r_ps = b.ps2.next()
            for k in range(nk):
                for (s0, sn, _) in segs:
                    P.add("pe", lambda e, wv=wv, k=k, j=j, s0=s0, sn=sn, ps=ps: e.matmul(
                        ps[:, s0:s0 + sn], lhsT=wv[:, k, j * 128:(j + 1) * 128], rhs=rhsT[:, k, s0:s0 + sn],
                        start=(k == 0), stop=(k == nk - 1)), reads=[r_w] + _aslist(r_rhs), writes=[r_ps])
            evac(m, ps, r_ps)


def emit_residual_evac(b, xin_d, xout_d, r_xout, tile, G, m, ps, r_ps):
    P = b.P
    tok0, segs = tile
    n = tile_n(tile)
    (Gt, r_G) = G
    xk, r_xk = b.xk.next()
    rd = [r_xout[m]] if xin_d is xout_d else []
    P.add("sp", lambda e: e.dma_start(out=xk[:, 0:n], in_=xin_d[m * 128:(m + 1) * 128, tok0:tok0 + n]),
          reads=rd, writes=[r_xk], dma=True)
    o, r_o = b.tmpf.next()
    for (c0, cn, w) in segs:
        P.add("dve", lambda e, c0=c0, cn=cn, w=w: e.scalar_tensor_tensor(
            out=o[:, c0:c0 + cn], in0=ps[:, c0:c0 + cn], scalar=Gt[:, m, w:w + 1], in1=xk[:, c0:c0 + cn],
            op0=ALU.mult, op1=ALU.add), reads=[r_ps, r_xk, r_G], writes=[r_o])
    P.add("sp", lambda e: e.dma_start(out=xout_d[m * 128:(m + 1) * 128, tok0:tok0 + n], in_=o[:, 0:n]),
          reads=[r_o], writes=[r_xout[m]], dma=True)


def emit_ffn(b, tile, hT, r_h, hid, r_hid, xin_d, xout_d, r_xout, w1_d, w2_d, G):
    P = b.P
    n = tile_n(tile)
    for half in range(2):
        def evac1(m, ps, r_ps):
            tm, r_tm = b.tmpf.next()
            P.add("act", lambda e: e.activation(out=tm[:, 0:n], in_=ps[:, 0:n], func=AF.Relu),
                  reads=[r_ps], writes=[r_tm])
            P.add("dve", lambda e: e.tensor_tensor(out=hid[:, m, 0:n], in0=tm[:, 0:n], in1=tm[:, 0:n], op=ALU.mult),
                  reads=[r_tm], writes=_aslist(r_hid))
        emit_linear_fm(b, w1_d, 0, KC, half * 4096, 32, hT, r_h, tile, evac1, blk_chunks=4)

        def evac2(m, ps, r_ps, half=half):
            emit_residual_evac(b, xin_d if half == 0 else xout_d, xout_d, r_xout, tile, G, m, ps, r_ps)
        emit_linear_fm(b, w2_d, half * 32, 32, 0, KC, hid, r_hid, tile, evac2, blk_chunks=2)


def emit_final_norm(b, x_d, out_d, tile, fg, r_fg):
    P = b.P
    tok0, segs = tile
    n = tile_n(tile)
    emit_rms_stats(b, x_d, tile)
    for k in range(KC):
        xk, r_xk = b.xk.next()
        P.add("sp", lambda e, xk=xk, k=k: e.dma_start(out=xk[:, 0:n], in_=x_d[k * 128:(k + 1) * 128, tok0:tok0 + n]),
              writes=[r_xk], dma=True)
        tm, r_tm = b.tmpf.next()
        P.add("dve", lambda e, xk=xk, tm=tm, k=k: e.scalar_tensor_tensor(
            out=tm[:, 0:n], in0=xk[:, 0:n], scalar=fg[:, k:k + 1], in1=b.Rt[:, 0:n], op0=ALU.mult, op1=ALU.mult),
            reads=[r_xk, b.r_R, r_fg], writes=[r_tm])
        P.add("sp", lambda e, tm=tm, k=k: e.dma_start(out=out_d[k * 128:(k + 1) * 128, tok0:tok0 + n], in_=tm[:, 0:n]),
              reads=[r_tm], dma=True)


def emit_gmlp(b, tile, hT, r_h, uT, r_u, vt, r_v, xin_d, xout_d, r_xout, W, G):
    P = b.P
    tok0, segs = tile
    n = tile_n(tile)
    ntb = n // 128
    w_in = W["w_in"]

    def evac_u(m, ps, r_ps):
        P.add("act", lambda e: e.activation(out=uT[:, m, 0:n], in_=ps[:, 0:n], func=AF.Gelu_apprx_tanh,
                                            bias=W["b_u"][:, m:m + 1], scale=1.0),
              reads=[r_ps, W["r_c"]], writes=[r_u])
    emit_linear_fm(b, w_in, 0, KC, 0, 16, hT, r_h, tile, evac_u, blk_chunks=4)

    ssum = b.sb([128, 6, 4], F32)
    ssq = b.sb([128, 6, 4], F32)
    r_st = Res("stats")
    for nb in range(4):
        wv, r_w = b.wload(wview(w_in, 0, KC, 2048 + nb * 512, 512), KC, 512)
        for tb in range(ntb):
            ps, r_ps = b.ps2.next()
            for k in range(KC):
                P.add("pe", lambda e, wv=wv, k=k, tb=tb, ps=ps: e.matmul(
                    ps[:, 0:512], lhsT=hT[:, k, tb * 128:(tb + 1) * 128], rhs=wv[:, k, :],
                    start=(k == 0), stop=(k == KC - 1)), reads=[r_w, r_h], writes=[r_ps])
            tm, r_tm = b.tmpf.next()
            P.add("dve", lambda e, ps=ps, tm=tm, nb=nb: e.tensor_tensor(
                out=tm[:, 0:512], in0=ps[:, 0:512], in1=W["b_v"][:, nb * 512:(nb + 1) * 512], op=ALU.add),
                reads=[r_ps, W["r_c"]], writes=[r_tm])
            P.add("act", lambda e, tm=tm, tb=tb, nb=nb: e.activation(
                out=vt[:, tb, nb * 512:(nb + 1) * 512], in_=tm[:, 0:512], func=AF.Gelu_apprx_tanh,
                accum_out=ssum[:, tb, nb:nb + 1]), reads=[r_tm], writes=[r_v, r_st])
            sq, r_sq = b.sqb.next()
            P.add("act", lambda e, sq=sq, tb=tb, nb=nb: e.activation(
                out=sq[:, 0:512], in_=vt[:, tb, nb * 512:(nb + 1) * 512], func=AF.Square,
                accum_out=ssq[:, tb, nb:nb + 1]), reads=[r_v], writes=[r_sq, r_st])
    mean = b.sb([128, 6], F32)
    ex2 = b.sb([128, 6], F32)
    rstd = b.sb([128, 6], F32)

    def add4(dst, src):
        P.add("dve", lambda e: e.tensor_tensor(out=dst[:, 0:ntb], in0=src[:, 0:ntb, 0], in1=src[:, 0:ntb, 1], op=ALU.add),
              reads=[r_st], writes=[r_st])
        for j in (2, 3):
            P.add("dve", lambda e, j=j: e.tensor_tensor(out=dst[:, 0:ntb], in0=dst[:, 0:ntb], in1=src[:, 0:ntb, j], op=ALU.add),
                  reads=[r_st], writes=[r_st])
    add4(mean, ssum)
    add4(ex2, ssq)
    P.add("dve", lambda e: e.tensor_scalar(out=mean[:, 0:ntb], in0=mean[:, 0:ntb], scalar1=1.0 / 2048, scalar2=None, op0=ALU.mult),
          reads=[r_st], writes=[r_st])
    P.add("dve", lambda e: e.tensor_tensor(out=rstd[:, 0:ntb], in0=mean[:, 0:ntb], in1=mean[:, 0:ntb], op=ALU.mult),
          reads=[r_st], writes=[r_st])
    P.add("dve", lambda e: e.scalar_tensor_tensor(out=rstd[:, 0:ntb], in0=ex2[:, 0:ntb], scalar=1.0 / 2048, in1=rstd[:, 0:ntb],
                                                  op0=ALU.mult, op1=ALU.subtract), reads=[r_st], writes=[r_st])
    P.add("act", lambda e: e.activation(out=rstd[:, 0:ntb], in_=rstd[:, 0:ntb], func=AF.Sqrt, scale=1.0,
                                        bias=b.epst[:, 0:1]), reads=[r_st, b.r_ones], writes=[r_st])
    P.add("dve", lambda e: e.reciprocal(out=rstd[:, 0:ntb], in_=rstd[:, 0:ntb]), reads=[r_st], writes=[r_st])
    for tb in range(ntb):
        P.add("dve", lambda e, tb=tb: e.tensor_scalar(
            out=vt[:, tb, :], in0=vt[:, tb, :], scalar1=mean[:, tb:tb + 1], scalar2=rstd[:, tb:tb + 1],
            op0=ALU.subtract, op1=ALU.mult), reads=[r_v, r_st], writes=[r_v])
    for g in range(16):
        ps, r_ps = b.ps2.next()
        for tb in range(ntb):
            P.add("pe", lambda e, g=g, tb=tb, ps=ps: e.matmul(
                ps[:, tb * 128:(tb + 1) * 128], lhsT=vt[:, tb, g * 128:(g + 1) * 128], rhs=W["wsT"][:, g, :],
                start=True, stop=True), reads=[r_v, W["r_c"]], writes=[r_ps])
        tm, r_tm = b.tmpf.next()
        P.add("dve", lambda e, g=g, ps=ps, tm=tm: e.scalar_tensor_tensor(
            out=tm[:, 0:n].rearrange("p (a c) -> p a c", c=128),
            in0=ps[:, 0:n].rearrange("p (a c) -> p a c", c=128),
            scalar=W["ln_g"][:, g:g + 1],
            in1=W["b_s"][:, g:g + 1, :].broadcast_to([128, ntb, 128]),
            op0=ALU.mult, op1=ALU.add), reads=[r_ps, W["r_c"]], writes=[r_tm])
        P.add("dve", lambda e, g=g, tm=tm: e.tensor_tensor(out=uT[:, g, 0:n], in0=uT[:, g, 0:n], in1=tm[:, 0:n], op=ALU.mult),
              reads=[r_tm, r_u], writes=[r_u])

    def evac_o(m, ps, r_ps):
        emit_residual_evac(b, xin_d, xout_d, r_xout, tile, G, m, ps, r_ps)
    emit_linear_fm(b, W["w_out"], 0, KC, 0, 16, uT, r_u, tile, evac_o, blk_chunks=4)


def build_stage_mod():
    b = Builder()
    P = b.P
    wm = b.inp("wm", [4, D, 1536])
    bm = b.inp("bm", [128, 48])
    cT = b.inp("cT", [128, 16, 3])
    o = b.out("modo", [128, 48, 3])
    ct, r_ct = b.load_small(cT, [128, 16, 3])
    bt, r_bt = b.load_small(bm, [128, 48])
    sT = b.sb([128, 16, 3], F32)
    r_s = Res()
    P.add("act", lambda e: e.activation(out=sT[:], in_=ct[:], func=AF.Silu), reads=[r_ct], writes=[r_s])
    wsl = b.rot_sb(3, [128, 16, 512], F32)
    pss = b.rot_ps(4, [128, 512])
    ot = b.sb([128, 48, 3], F32)
    r_o = Res()
    for l in range(4):
        for blk in range(3):
            wt, r_w = wsl.next()
            P.add("sp", lambda e, wt=wt, l=l, blk=blk: e.dma_start(
                out=wt[:], in_=wm[l, :, blk * 512:(blk + 1) * 512].rearrange("(k p) c -> p k c", p=128)),
                writes=[r_w], dma=True)
            for j in range(4):
                mc = blk * 4 + j
                ps, r_ps = pss.next()
                for k in range(16):
                    P.add("pe", lambda e, wt=wt, k=k, j=j, ps=ps: e.matmul(
                        ps[:, 0:3], lhsT=wt[:, k, j * 128:(j + 1) * 128], rhs=sT[:, k, :],
                        start=(k == 0), stop=(k == 15)), reads=[r_w, r_s], writes=[r_ps])
                P.add("dve", lambda e, ps=ps, l=l, mc=mc: e.tensor_scalar(
                    out=ot[:, l * 12 + mc, :], in0=ps[:, 0:3], scalar1=bt[:, l * 12 + mc:l * 12 + mc + 1], scalar2=None,
                    op0=ALU.add), reads=[r_ps, r_bt], writes=[r_o])
    P.add("sp", lambda e: e.dma_start(out=o, in_=ot[:]), reads=[r_o], dma=True)
    return b


def build_stage_gmlp_layer(with_ctx, final):
    b = Builder()
    P = b.P
    ntok = 2304 if with_ctx else 2048
    xin = b.inp("xT", [D, ntok])
    modv = b.inp("modv", [128, 96, 2])
    n1g = b.inp("n1g", [128, 16])
    n2g = b.inp("n2g", [128, 16])
    w_in = b.inp("w_in", [D, 4096])
    b_u = b.inp("b_u", [128, 16])
    b_v = b.inp("b_v", [128, 2048])
    ln_g = b.inp("ln_g", [128, 16])
    wsT = b.inp("wsT", [128, 16, 128])
    b_s = b.inp("b_s", [128, 16, 128])
    w_out = b.inp("w_out", [D, D])
    w1 = b.inp("w1", [D, 8192])
    w2 = b.inp("w2", [8192, D])
    if final:
        fgd = b.inp("fg", [128, 16])
        xo2 = b.scratch("xo2", [D, ntok])
        outd = b.out("xo", [D, ntok])
    else:
        xo2 = b.out("xo", [D, ntok])
    xmid = b.scratch("xmid", [D, ntok])
    b.setup_common()
    W = {"w_in": w_in, "w_out": w_out, "r_c": Res("consts")}
    rc = W["r_c"]
    for nm, src, shape, dt, cast in (("b_u", b_u, [128, 16], F32, False), ("b_v", b_v, [128, 2048], F32, False),
                                     ("ln_g", ln_g, [128, 16], F32, False), ("wsT", wsT, [128, 16, 128], BF16, True),
                                     ("b_s", b_s, [128, 16, 128], F32, False)):
        t = b.sb(shape, dt)
        P.add("pool" if cast else "sp", lambda e, t=t, src=src: e.dma_start(out=t[:], in_=src), writes=[rc], dma=True)
        W[nm] = t
    A1, B1, G1 = emit_modprep(b, modv, n1g, 0, 1, 2)
    A2, B2, G2 = emit_modprep(b, modv, n2g, 3, 4, 5)
    if final:
        fg, r_fg = b.load_small(fgd, [128, 16])
    hT = b.sb([128, 16, TT], BF16)
    r_h = Res("hT")
    big = b.sb([128, 32 * TT], BF16)
    uT = big[:, 0:16 * TT].rearrange("p (k t) -> p k t", k=16)
    r_u = Res("uT")
    vt = big[:, 16 * TT:32 * TT].rearrange("p (a f) -> p a f", a=6)
    r_v = Res("vt")
    hid = big[:, :].rearrange("p (k t) -> p k t", k=32)
    r_hid = [r_u, r_v]
    for ti, tile in enumerate(tiles_for(with_ctx)):
        r_xmid = [Res() for _ in range(KC)]
        r_xo = [Res() for _ in range(KC)]
        emit_norm_mod(b, xin, tile, A1, B1, hT, r_h)
        emit_gmlp(b, tile, hT, r_h, uT, r_u, vt, r_v, xin, xmid, r_xmid, W, G1)
        emit_norm_mod_dep(b, xmid, r_xmid, tile, A2, B2, hT, r_h)
        emit_ffn(b, tile, hT, r_h, hid, r_hid, xmid, xo2, r_xo, w1, w2, G2)
        if final:
            emit_final_norm_dep(b, xo2, r_xo, outd, tile, fg, r_fg)
    return b


class _DepView:
    pass


def emit_norm_mod_dep(b, x_d, r_x, tile, A, Bv, hT, r_h):
    P = b.P
    orig = P.add

    def add(st, fn, reads=(), writes=(), dma=False):
        if dma and st == "sp":
            reads = list(reads) + list(r_x)
        return orig(st, fn, reads=reads, writes=writes, dma=dma)
    P.add = add
    try:
        emit_norm_mod(b, x_d, tile, A, Bv, hT, r_h)
    finally:
        P.add = orig


def emit_final_norm_dep(b, x_d, r_x, out_d, tile, fg, r_fg):
    P = b.P
    orig = P.add

    def add(st, fn, reads=(), writes=(), dma=False):
        if dma and st == "sp" and not reads:
            reads = list(r_x)
        return orig(st, fn, reads=reads, writes=writes, dma=dma)
    P.add = add
    try:
        emit_final_norm(b, x_d, out_d, tile, fg, r_fg)
    finally:
        P.add = orig


def _split512(n):
    out = []
    c = 0
    while c < n:
        out.append((c, min(512, n - c)))
        c += 512
    return out


def emit_rope_evac(b, ps, r_ps, n_lat, n, DR, prot, cosT, sinT, r_cs, dst_d, w_res=()):
    P = b.P
    ob, r_ob = b.obf.next()
    if n_lat > 0:
        qf, r_qf = b.tmpf.next()
        P.add("act", lambda e: e.activation(out=qf[0:DR, 0:n_lat], in_=ps[0:DR, 0:n_lat], func=AF.Identity),
              reads=[r_ps], writes=[r_qf])
        qb, r_qb = b.sqb.next()
        P.add("dve", lambda e: e.tensor_copy(out=qb[0:DR, 0:n_lat], in_=qf[0:DR, 0:n_lat]), reads=[r_qf], writes=[r_qb])
        pr, r_pr = b.ps2.next()
        for (s0, sn) in _split512(n_lat):
            P.add("pe", lambda e, s0=s0, sn=sn: e.matmul(pr[0:DR, s0:s0 + sn], lhsT=prot[0:DR, 0:DR], rhs=qb[0:DR, s0:s0 + sn],
                                                         start=True, stop=True), reads=[r_qb, r_cs], writes=[r_pr])
        P.add("dve", lambda e: e.tensor_tensor(out=qf[0:DR, 0:n_lat], in0=qf[0:DR, 0:n_lat], in1=cosT[0:DR, 0:n_lat], op=ALU.mult),
              reads=[r_qf, r_cs], writes=[r_qf])
        t2, r_t2 = b.tmpf.next()
        P.add("dve", lambda e: e.tensor_tensor(out=t2[0:DR, 0:n_lat], in0=pr[0:DR, 0:n_lat], in1=sinT[0:DR, 0:n_lat], op=ALU.mult),
              reads=[r_pr, r_cs], writes=[r_t2])
        P.add("dve", lambda e: e.tensor_tensor(out=ob[0:DR, 0:n_lat], in0=qf[0:DR, 0:n_lat], in1=t2[0:DR, 0:n_lat], op=ALU.add),
              reads=[r_qf, r_t2], writes=[r_ob])
    if n > n_lat:
        P.add("act", lambda e: e.activation(out=ob[0:DR, n_lat:n], in_=ps[0:DR, n_lat:n], func=AF.Identity),
              reads=[r_ps], writes=[r_ob])
    P.add("sp", lambda e: e.dma_start(out=dst_d, in_=ob[0:DR, 0:n]), reads=[r_ob], writes=list(w_res), dma=True)


def emit_copy_evac(b, ps, r_ps, rows, n, dst_d, w_res=()):
    P = b.P
    ob, r_ob = b.obf.next()
    P.add("act", lambda e: e.activation(out=ob[0:rows, 0:n], in_=ps[0:rows, 0:n], func=AF.Identity), reads=[r_ps], writes=[r_ob])
    P.add("sp", lambda e: e.dma_start(out=dst_d, in_=ob[0:rows, 0:n]), reads=[r_ob], writes=list(w_res), dma=True)


def n_latent(tile):
    return sum(s[1] for s in tile[1] if s[2] == 0)


def load_rope_tiles(b, cos_d, sin_d, prot_d, DR):
    cosT = b.sb([128, TT], F32)
    sinT = b.sb([128, TT], F32)
    prot = b.sb([128, 128], BF16)
    r_cs = Res("cs")
    b.P.add("pool", lambda e: e.dma_start(out=prot[0:DR, 0:DR], in_=prot_d), writes=[r_cs], dma=True)
    return cosT, sinT, prot, r_cs


def emit_load_cs(b, cosT, sinT, r_cs, cos_d, sin_d, DR, tok0, nl):
    b.P.add("sp", lambda e: e.dma_start(out=cosT[0:DR, 0:nl], in_=cos_d[:, tok0:tok0 + nl]), writes=[r_cs], dma=True)
    b.P.add("sp", lambda e: e.dma_start(out=sinT[0:DR, 0:nl], in_=sin_d[:, tok0:tok0 + nl]), writes=[r_cs], dma=True)


def emit_v_tokmajor(b, w_d, k0, nk, c0, ncols_total, lhsT_t, r_l, tile, V_d, vcol0):
    P = b.P
    tok0, segs = tile
    n = tile_n(tile)
    for nb in range(ncols_total // 512):
        wv, r_w = b.wload(wview(w_d, k0, nk, c0 + nb * 512, 512), nk, 512)
        for tb in range(n // 128):
            ps, r_ps = b.ps2.next()
            for k in range(nk):
                P.add("pe", lambda e, wv=wv, k=k, tb=tb, ps=ps: e.matmul(
                    ps[:, 0:512], lhsT=lhsT_t[:, k, tb * 128:(tb + 1) * 128], rhs=wv[:, k, :],
                    start=(k == 0), stop=(k == nk - 1)), reads=[r_w, r_l], writes=[r_ps])
            emit_copy_evac(b, ps, r_ps, 128, 512,
                           V_d[tok0 + tb * 128:tok0 + (tb + 1) * 128, vcol0 + nb * 512:vcol0 + (nb + 1) * 512])


def build_stage_diff_qkv():
    b = Builder()
    P = b.P
    xin = b.inp("xT", [D, 2304])
    modv = b.inp("modv", [128, 96, 2])
    n1g = b.inp("n1g", [128, 16])
    w_qkv = b.inp("w_qkv", [D, 6144])
    cos_d = b.inp("cosT", [128, 2048])
    sin_d = b.inp("sinT", [128, 2048])
    prot_d = b.inp("prot", [128, 128])
    QT = b.out("QT", [16, 128, 2304], BF16)
    KT = b.out("KT", [16, 128, 2304], BF16)
    V = b.out("V", [2304, 2048], BF16)
    b.setup_common()
    b.obf = b.rot_sb(3, [128, TT], BF16)
    A1, B1, G1 = emit_modprep(b, modv, n1g, 0, 1, 2)
    cosT, sinT, prot, r_cs = load_rope_tiles(b, cos_d, sin_d, prot_d, 128)
    hT = b.sb([128, 16, TT], BF16)
    r_h = Res("hT")
    for tile in tiles_for(True):
        tok0, segs = tile
        n = tile_n(tile)
        nl = n_latent(tile)
        emit_norm_mod(b, xin, tile, A1, B1, hT, r_h)
        emit_load_cs(b, cosT, sinT, r_cs, cos_d, sin_d, 128, tok0, nl)

        def evac(m, ps, r_ps):
            dst = (QT if m < 16 else KT)[m % 16, :, tok0:tok0 + n]
            if DEBUG_MODE == "norope":
                emit_copy_evac(b, ps, r_ps, 128, n, dst)
            else:
                emit_rope_evac(b, ps, r_ps, nl, n, 128, prot, cosT, sinT, r_cs, dst)
        emit_linear_fm(b, w_qkv, 0, KC, 0, 32, hT, r_h, tile, evac)
        if DEBUG_MODE != "nov":
            emit_v_tokmajor(b, w_qkv, 0, KC, 4096, 2048, hT, r_h, tile, V, 0)
    return b


KG = [(0, 22), (22, 22), (44, 22)]
KG_CTX = [(64, 2)]


def setup_attn_buffers(b):
    b.big = b.sb([128, 32 * TT], BF16)
    b.hT = b.sb([128, 16, TT], BF16)
    b.r_h = Res("hT")
    b.r_p0 = Res("piece0")
    b.r_p1 = Res("piece1")
    b.hid = b.big[:, :].rearrange("p (k t) -> p k t", k=32)
    b.r_hid = [b.r_p0, b.r_p1]
    pieces = [b.big[:, 0:12288], b.big[:, 12288:24576], b.hT[:, :, :].rearrange("p k t -> p (k t)")]
    b.pieces = Rot(pieces)
    b.pieces.res = [b.r_p0, b.r_p1, b.r_h]
    b.pt = b.rot_sb(3, [128, 512], BF16)
    b.ef = b.rot_sb(6, [128, 512], F32)
    b.obf = b.rot_sb(3, [128, TT], BF16)


def emit_diff_attn(b, h, q0, qn, groups, Qh, r_q, KT_all, V_all, aoT, r_ao_w, neglam, gs, r_sc, scale):
    P = b.P
    T = b.ps2.tiles
    bank = b.bank
    O = [[(T[j][:, eh * 512:eh * 512 + qn], bank[2 * j + eh]) for eh in range(2)] for j in range(2)]
    L = [(T[2][:, j * 512:j * 512 + qn], bank[4 + j]) for j in range(2)]
    Sc = [(T[3][:, i * 512:i * 512 + qn], bank[6 + i]) for i in range(2)]
    items = []
    for gi, (kc0, nkc) in enumerate(groups):
        buf, r_pc = b.pieces.next()
        Kp = [buf[:, j * 2816:j * 2816 + nkc * 128] for j in range(2)]
        Vp = buf[:, 5632:5632 + nkc * 256].rearrange("p (c e) -> p c e", e=256)
        for j in range(2):
            P.add("sp", lambda e, j=j: e.dma_start(out=Kp[j], in_=KT_all[h * 2 + j, :, kc0 * 128:(kc0 + nkc) * 128]),
                  writes=[r_pc], dma=True)
        P.add("sp", lambda e: e.dma_start(
            out=Vp, in_=V_all[kc0 * 128:(kc0 + nkc) * 128, h * 256:(h + 1) * 256].rearrange("(c p) e -> p c e", p=128)),
            writes=[r_pc], dma=True)
        for j in range(2):
            for c in range(nkc):
                first = (gi == 0 and c == 0)
                last = (gi == len(groups) - 1 and c == nkc - 1)
                items.append((Kp[j], Vp, r_pc, j, c, first, last))
    nit = len(items)

    def score(i):
        Kpj, Vp, r_pc, j, c, first, last = items[i]
        sc, r_s = Sc[i % 2]
        P.add("pe", lambda e: e.matmul(sc, lhsT=Kpj[:, c * 128:(c + 1) * 128], rhs=Qh[:, j, q0:q0 + qn], start=True, stop=True),
              reads=[r_pc, r_q], writes=[r_s])
    score(0)
    for i in range(nit):
        if i + 1 < nit:
            score(i + 1)
        Kpj, Vp, r_pc, j, c, first, last = items[i]
        sc, r_s = Sc[i % 2]
        pt, r_pt = b.pt.next()
        P.add("act", lambda e: e.activation(out=pt[:, 0:qn], in_=sc, func=AF.Exp, scale=scale), reads=[r_s], writes=[r_pt])
        for eh in range(2):
            P.add("pe", lambda e, eh=eh: e.matmul(O[j][eh][0], lhsT=Vp[:, c, eh * 128:(eh + 1) * 128], rhs=pt[:, 0:qn],
                                                  start=first, stop=last), reads=[r_pc, r_pt], writes=[O[j][eh][1]])
        P.add("pe", lambda e: e.matmul(L[j][0], lhsT=b.ones[:, :], rhs=pt[:, 0:qn], start=first, stop=last),
              reads=[r_pt, b.r_ones], writes=[L[j][1]])
    rl = []
    for j in range(2):
        t, r_t = b.ef.next()
        P.add("dve", lambda e, j=j, t=t: e.reciprocal(out=t[:, 0:qn], in_=L[j][0]), reads=[L[j][1]], writes=[r_t])
        rl.append((t, r_t))
    oe = []
    for eh in range(2):
        t1, r_t1 = b.ef.next()
        P.add("dve", lambda e, eh=eh, t1=t1: e.tensor_tensor(out=t1[:, 0:qn], in0=O[0][eh][0], in1=rl[0][0][:, 0:qn], op=ALU.mult),
              reads=[O[0][eh][1], rl[0][1]], writes=[r_t1])
        t2, r_t2 = b.ef.next()
        P.add("dve", lambda e, eh=eh, t2=t2: e.tensor_tensor(out=t2[:, 0:qn], in0=O[1][eh][0], in1=rl[1][0][:, 0:qn], op=ALU.mult),
              reads=[O[1][eh][1], rl[1][1]], writes=[r_t2])
        P.add("dve", lambda e, t1=t1, t2=t2: e.scalar_tensor_tensor(
            out=t1[:, 0:qn], in0=t2[:, 0:qn], scalar=neglam[:, 0:1], in1=t1[:, 0:qn], op0=ALU.mult, op1=ALU.add),
            reads=[r_t1, r_t2, r_sc], writes=[r_t1])
        oe.append((t1, r_t1))
    ssb, r_ssb = Sc[0]
    for eh in range(2):
        sq, r_sq = b.pt.next()
        P.add("act", lambda e, eh=eh, sq=sq: e.activation(out=sq[:, 0:qn], in_=oe[eh][0][:, 0:qn], func=AF.Square),
              reads=[oe[eh][1]], writes=[r_sq])
        P.add("pe", lambda e, eh=eh, sq=sq: e.matmul(ssb, lhsT=b.ones[:, :], rhs=sq[:, 0:qn], start=(eh == 0), stop=(eh == 1)),
              reads=[r_sq, b.r_ones], writes=[r_ssb])
    rs, r_rs = rl[0]
    P.add("act", lambda e: e.activation(out=rs[:, 0:qn], in_=ssb, func=AF.Sqrt, scale=1.0 / 256, bias=b.epst[:, 0:1]),
          reads=[r_ssb, b.r_ones], writes=[r_rs])
    P.add("dve", lambda e: e.reciprocal(out=rs[:, 0:qn], in_=rs[:, 0:qn]), reads=[r_rs], writes=[r_rs])
    for eh in range(2):
        ob, r_ob = b.obf.next()
        P.add("dve", lambda e, eh=eh, ob=ob: e.scalar_tensor_tensor(
            out=ob[:, 0:qn], in0=oe[eh][0][:, 0:qn], scalar=gs[:, eh:eh + 1], in1=rs[:, 0:qn], op0=ALU.mult, op1=ALU.mult),
            reads=[oe[eh][1], r_rs, r_sc], writes=[r_ob])
        row = h * 256 + eh * 128
        P.add("sp", lambda e, ob=ob, row=row: e.dma_start(out=aoT[row:row + 128, q0:q0 + qn], in_=ob[:, 0:qn]),
              reads=[r_ob], writes=[r_ao_w[h * 2 + eh]], dma=True)


def emit_oproj_ffn(b, tiles, aoT, r_ao, w_o, xin, xmid, xo, w1, w2, A2, B2, G1, G2, qblocks, final=None):
    P = b.P
    for tile in tiles:
        tok0, segs = tile
        n = tile_n(tile)
        r_xmid = [Res() for _ in range(KC)]
        r_xo = [Res() for _ in range(KC)]
        for k in range(KC):
            deps = [r_ao[k][qi] for qi, (q0, qn) in enumerate(qblocks) if q0 < tok0 + n and q0 + qn > tok0]
            P.add("sp", lambda e, k=k: e.dma_start(out=b.hT[:, k, 0:n], in_=aoT[k * 128:(k + 1) * 128, tok0:tok0 + n]),
                  reads=deps, writes=[b.r_h], dma=True)

        def evac_o(m, ps, r_ps):
            emit_residual_evac(b, xin, xmid, r_xmid, tile, G1, m, ps, r_ps)
        emit_linear_fm(b, w_o, 0, KC, 0, 16, b.hT, b.r_h, tile, evac_o)
        emit_norm_mod_dep(b, xmid, r_xmid, tile, A2, B2, b.hT, b.r_h)
        emit_ffn(b, tile, b.hT, b.r_h, b.hid, b.r_hid, xmid, xo, r_xo, w1, w2, G2)


def build_stage_diff_attn(lambda_init):
    b = Builder()
    P = b.P
    xin = b.inp("xT", [D, 2304])
    modv = b.inp("modv", [128, 96, 2])
    n2g = b.inp("n2g", [128, 16])
    QT = b.inp("QT", [16, 128, 2304], BF16)
    KT_all = b.inp("KT_all", [16, 128, 8448], BF16)
    V_all = b.inp("V_all", [8448, 2048], BF16)
    lamv = b.inp("lamv", [128, 4, 128])
    sg_d = b.inp("subg", [128, 2])
    w_o = b.inp("w_o", [D, D])
    w1 = b.inp("w1", [D, 8192])
    w2 = b.inp("w2", [8192, D])
    xo = b.out("xo", [D, 2304])
    if DEBUG_MODE == "dbg":
        xmid = b.out("xmid", [D, 2304])
        aoT = b.out("aoT", [D, 2304], BF16)
    else:
        xmid = b.scratch("xmid", [D, 2304])
        aoT = b.scratch("aoT", [D, 2304], BF16)
    b.setup_common()
    setup_attn_buffers(b)
    A2, B2, G2 = emit_modprep(b, modv, n2g, 3, 4, 5)
    G1 = (b.last_modv[0][:, 2 * 16:3 * 16, :], b.last_modv[1])
    lam_t, r_lam = b.load_small(lamv, [128, 4, 128])
    sg, r_sg = b.load_small(sg_d, [128, 2])
    r_sc = Res("scal")
    prod = b.sb([128, 128], F32)
    ssum = b.sb([128, 2], F32)
    neglam = b.sb([128, 1], F32)
    gs = b.sb([128, 2], F32)
    for i in range(2):
        P.add("dve", lambda e, i=i: e.tensor_tensor(out=prod[:, :], in0=lam_t[:, 2 * i, :], in1=lam_t[:, 2 * i + 1, :], op=ALU.mult),
              reads=[r_lam], writes=[r_sc])
        P.add("act", lambda e, i=i: e.activation(out=prod[:, :], in_=prod[:, :], func=AF.Identity, accum_out=ssum[:, i:i + 1]),
              reads=[r_sc], writes=[r_sc])
    P.add("act", lambda e: e.activation(out=ssum[:, :], in_=ssum[:, :], func=AF.Exp), reads=[r_sc], writes=[r_sc])
    P.add("dve", lambda e: e.tensor_tensor(out=neglam[:, :], in0=ssum[:, 1:2], in1=ssum[:, 0:1], op=ALU.subtract),
          reads=[r_sc], writes=[r_sc])
    P.add("dve", lambda e: e.tensor_scalar(out=neglam[:, :], in0=neglam[:, :], scalar1=-float(lambda_init), scalar2=None, op0=ALU.add),
          reads=[r_sc], writes=[r_sc])
    P.add("dve", lambda e: e.tensor_scalar(out=gs[:, :], in0=sg[:, :], scalar1=float(1.0 - lambda_init), scalar2=None, op0=ALU.mult),
          reads=[r_sg, r_sc], writes=[r_sc])
    qblocks = [(0, 512), (512, 512), (1024, 512), (1536, 512), (2048, 256)]
    r_ao = [[Res() for _ in qblocks] for _ in range(KC)]
    Qrot = b.rot_sb(2, [128, 2, 2304], BF16)
    scale = 128 ** -0.5
    for h in range(8):
        Qh, r_q = Qrot.next()
        P.add("sp", lambda e, h=h, Qh=Qh: e.dma_start(out=Qh[:], in_=QT[2 * h:2 * h + 2].rearrange("j p t -> p j t")),
              writes=[r_q], dma=True)
        for qi, (q0, qn) in enumerate(qblocks):
            groups = KG if q0 < 2048 else KG_CTX
            r_w = [r_ao[k][qi] for k in range(KC)]
            emit_diff_attn(b, h, q0, qn, groups, Qh, r_q, KT_all, V_all, aoT, r_w, neglam, gs, r_sc, scale)
    emit_oproj_ffn(b, tiles_for(True), aoT, r_ao, w_o, xin, xmid, xo, w1, w2, A2, B2, G1, G2, qblocks)
    return b


def build_stage_mla_qkv():
    b = Builder()
    P = b.P
    xin = b.inp("xT", [D, 2304])
    modv = b.inp("modv", [128, 96, 2])
    n1g = b.inp("n1g", [128, 16])
    w_dqkv = b.inp("w_dqkv", [D, 1024])
    qg_d = b.inp("qg", [128, 4])
    kvg_d = b.inp("kvg", [128, 4])
    w_uq = b.inp("w_uq", [448, 3072])
    w_ukv = b.inp("w_ukv", [512, 4096])
    cos_d = b.inp("cosT", [64, 2048])
    sin_d = b.inp("sinT", [64, 2048])
    prot_d = b.inp("prot", [64, 64])
    QnT = b.out("QnT", [16, 128, 2048], BF16)
    QrT = b.out("QrT", [16, 64, 2048], BF16)
    KnT = b.out("KnT", [16, 128, 2304], BF16)
    KrT = b.out("KrT", [64, 2304], BF16)
    V = b.out("V", [2304, 2048], BF16)
    b.setup_common(n_w=1)
    b.obf = b.rot_sb(3, [128, TT], BF16)
    A1, B1, G1 = emit_modprep(b, modv, n1g, 0, 1, 2)
    cosT, sinT, prot, r_cs = load_rope_tiles(b, cos_d, sin_d, prot_d, 64)
    qg, r_qg = b.load_small(qg_d, [128, 4])
    kvg, r_kvg = b.load_small(kvg_d, [128, 4])
    r_wc = Res("wconst")
    wdq = b.sb([128, 16, 1024], BF16)
    for i in range(2):
        P.add("pool", lambda e, i=i: e.dma_start(out=wdq[:, :, i * 512:(i + 1) * 512], in_=wview(w_dqkv, 0, 16, i * 512, 512)),
              writes=[r_wc], dma=True)
    wuq = b.sb([128, 4, 3072], BF16)
    P.add("pool", lambda e: e.dma_start(out=wuq[:, 0:3, :], in_=wview(w_uq, 0, 3, 0, 3072)), writes=[r_wc], dma=True)
    P.add("pool", lambda e: e.dma_start(out=wuq[0:64, 3, :], in_=w_uq[384:448, :]), writes=[r_wc], dma=True)
    wukv = b.sb([128, 4, 4096], BF16)
    for i in range(2):
        P.add("pool", lambda e, i=i: e.dma_start(out=wukv[:, :, i * 2048:(i + 1) * 2048], in_=wview(w_ukv, 0, 4, i * 2048, 2048)),
              writes=[r_wc], dma=True)
    hT = b.sb([128, 16, TT], BF16)
    r_h = Res("hT")
    cf = b.sb([128, 4, TT], F32)
    r_cf = Res("cf")
    sqs = b.sb([128, 4, TT], BF16)
    r_sqs = Res("sqs")
    cqn = b.sb([128, 4, TT], BF16)
    r_cqn = Res("cqn")
    ckvn = b.sb([128, 4, TT], BF16)
    r_ckvn = Res("ckvn")
    Rn = b.sb([128, TT], F32)
    r_Rn = Res("Rn")
    KQ = [128, 128, 128, 64]
    for tile in tiles_for(True):
        tok0, segs = tile
        n = tile_n(tile)
        nl = n_latent(tile)
        segs_lat = [s for s in segs if s[2] == 0]
        emit_norm_mod(b, xin, tile, A1, B1, hT, r_h)
        emit_load_cs(b, cosT, sinT, r_cs, cos_d, sin_d, 64, tok0, nl)

        def latent_part(col0, Ms, gt, r_g, nfeat, dst, r_dst):
            for i, M in enumerate(Ms):
                ps, r_ps = b.ps2.next()
                for k in range(KC):
                    for (s0, sn, _) in segs:
                        P.add("pe", lambda e, k=k, s0=s0, sn=sn, M=M, i=i: e.matmul(
                            ps[0:M, s0:s0 + sn], lhsT=wdq[:, k, col0 + i * 128:col0 + i * 128 + M], rhs=hT[:, k, s0:s0 + sn],
                            start=(k == 0), stop=(k == KC - 1)), reads=[r_wc, r_h], writes=[r_ps])
                P.add("act", lambda e, M=M, i=i: e.activation(out=cf[0:M, i, 0:n], in_=ps[0:M, 0:n], func=AF.Identity),
                      reads=[r_ps], writes=[r_cf])
                P.add("act", lambda e, M=M, i=i: e.activation(out=sqs[0:M, i, 0:n], in_=cf[0:M, i, 0:n], func=AF.Square),
                      reads=[r_cf], writes=[r_sqs])
            ps, r_ps = b.ps2.next()
            for i, M in enumerate(Ms):
                for (s0, sn, _) in segs:
                    P.add("pe", lambda e, s0=s0, sn=sn, M=M, i=i: e.matmul(
                        ps[:, s0:s0 + sn], lhsT=b.ones[0:M, :], rhs=sqs[0:M, i, s0:s0 + sn],
                        start=(i == 0), stop=(i == len(Ms) - 1)), reads=[r_sqs, b.r_ones], writes=[r_ps])
            P.add("act", lambda e: e.activation(out=Rn[:, 0:n], in_=ps[:, 0:n], func=AF.Sqrt, scale=1.0 / nfeat, bias=b.epst[:, 0:1]),
                  reads=[r_ps, b.r_ones], writes=[r_Rn])
            P.add("dve", lambda e: e.reciprocal(out=Rn[:, 0:n], in_=Rn[:, 0:n]), reads=[r_Rn], writes=[r_Rn])
            for i, M in enumerate(Ms):
                P.add("dve", lambda e, M=M, i=i: e.scalar_tensor_tensor(
                    out=dst[0:M, i, 0:n], in0=cf[0:M, i, 0:n], scalar=gt[0:M, i:i + 1], in1=Rn[0:M, 0:n],
                    op0=ALU.mult, op1=ALU.mult), reads=[r_cf, r_g, r_Rn], writes=[r_dst])
        latent_part(0, KQ, qg, r_qg, 448.0, cqn, r_cqn)
        latent_part(448, [128] * 4, kvg, r_kvg, 512.0, ckvn, r_ckvn)
        ps, r_ps = b.ps2.next()
        for k in range(KC):
            for (s0, sn, _) in segs:
                P.add("pe", lambda e, k=k, s0=s0, sn=sn: e.matmul(
                    ps[0:64, s0:s0 + sn], lhsT=wdq[:, k, 960:1024], rhs=hT[:, k, s0:s0 + sn],
                    start=(k == 0), stop=(k == KC - 1)), reads=[r_wc, r_h], writes=[r_ps])
        emit_rope_evac(b, ps, r_ps, nl, n, 64, prot, cosT, sinT, r_cs, KrT[:, tok0:tok0 + n])
        for h in range(16):
            for part in range(2):
                c0 = h * 192 + part * 128
                M = 128 if part == 0 else 64
                ps, r_ps = b.ps2.next()
                for k in range(4):
                    for (s0, sn, _) in segs_lat:
                        P.add("pe", lambda e, k=k, s0=s0, sn=sn, c0=c0, M=M: e.matmul(
                            ps[0:M, s0:s0 + sn], lhsT=wuq[0:KQ[k], k, c0:c0 + M], rhs=cqn[0:KQ[k], k, s0:s0 + sn],
                            start=(k == 0), stop=(k == 3)), reads=[r_wc, r_cqn], writes=[r_ps])
                if part == 0:
                    emit_copy_evac(b, ps, r_ps, 128, nl, QnT[h, :, tok0:tok0 + nl])
                else:
                    emit_rope_evac(b, ps, r_ps, nl, nl, 64, prot, cosT, sinT, r_cs, QrT[h, :, tok0:tok0 + nl])
            ps, r_ps = b.ps2.next()
            for k in range(4):
                for (s0, sn, _) in segs:
                    P.add("pe", lambda e, k=k, s0=s0, sn=sn, h=h: e.matmul(
                        ps[:, s0:s0 + sn], lhsT=wukv[:, k, h * 256:h * 256 + 128], rhs=ckvn[:, k, s0:s0 + sn],
                        start=(k == 0), stop=(k == 3)), reads=[r_wc, r_ckvn], writes=[r_ps])
            emit_copy_evac(b, ps, r_ps, 128, n, KnT[h, :, tok0:tok0 + n])
        for tb in range(n // 128):
            for hb in range(4):
                ps, r_ps = b.ps2.next()
                for k in range(4):
                    rhs = wukv[:, k, :].rearrange("p (h t c) -> p h t c", h=16, t=2)[:, 4 * hb:4 * hb + 4, 1, :]
                    P.add("pe", lambda e, k=k, tb=tb, rhs=rhs: e.matmul(
                        ps[:, 0:512].rearrange("p (h c) -> p h c", h=4), lhsT=ckvn[:, k, tb * 128:(tb + 1) * 128], rhs=rhs,
                        start=(k == 0), stop=(k == 3)), reads=[r_wc, r_ckvn], writes=[r_ps])
                emit_copy_evac(b, ps, r_ps, 128, 512, V[tok0 + tb * 128:tok0 + (tb + 1) * 128, hb * 512:(hb + 1) * 512])
    return b


def emit_mla_attn(b, h, q0, qn, groups, Qh, r_q, krs, r_kr, KnT_all, V_all, aoT, r_ao_w, scale, par):
    P = b.P
    T = b.ps2.tiles
    bank = b.bank
    O = (T[par][:, 0:qn], bank[2 * par])
    L = (T[par][:, 512:512 + qn], bank[2 * par + 1])
    Sc = [(T[3][:, i * 512:i * 512 + qn], bank[6 + i]) for i in range(2)]
    chunks = []
    for gi, (kc0, nkc) in enumerate(groups):
        buf, r_pc = b.pieces.next()
        Kp = buf[:, 0:nkc * 128]
        Vp = buf[:, 2816:2816 + nkc * 128].rearrange("p (c e) -> p c e", e=128)
        P.add("sp", lambda e: e.dma_start(out=Kp, in_=KnT_all[h, :, kc0 * 128:(kc0 + nkc) * 128]), writes=[r_pc], dma=True)
        P.add("sp", lambda e: e.dma_start(
            out=Vp, in_=V_all[kc0 * 128:(kc0 + nkc) * 128, h * 128:(h + 1) * 128].rearrange("(c p) e -> p c e", p=128)),
            writes=[r_pc], dma=True)
        for c in range(nkc):
            chunks.append((Kp, Vp, r_pc, c, kc0 + c))
    nch = len(chunks)

    def score(i):
        Kp, Vp, r_pc, c, gc = chunks[i]
        sc, r_s = Sc[i % 2]
        P.add("pe", lambda e: e.matmul(sc, lhsT=Kp[:, c * 128:(c + 1) * 128], rhs=Qh[:, 0, q0:q0 + qn], start=True, stop=False),
              reads=[r_pc, r_q], writes=[r_s])
        P.add("pe", lambda e: e.matmul(sc, lhsT=krs[0:64, gc * 128:(gc + 1) * 128], rhs=Qh[0:64, 1, q0:q0 + qn], start=False, stop=True),
              reads=[r_kr, r_q], writes=[r_s])
    score(0)
    for i in range(nch):
        if i + 1 < nch:
            score(i + 1)
        Kp, Vp, r_pc, c, gc = chunks[i]
        sc, r_s = Sc[i % 2]
        pt, r_pt = b.pt.next()
        P.add("act", lambda e: e.activation(out=pt[:, 0:qn], in_=sc, func=AF.Exp, scale=scale), reads=[r_s], writes=[r_pt])
        P.add("pe", lambda e: e.matmul(O[0], lhsT=Vp[:, c, :], rhs=pt[:, 0:qn], start=(i == 0), stop=(i == nch - 1)),
              reads=[r_pc, r_pt], writes=[O[1]])
        P.add("pe", lambda e: e.matmul(L[0], lhsT=b.ones[:, :], rhs=pt[:, 0:qn], start=(i == 0), stop=(i == nch - 1)),
              reads=[r_pt, b.r_ones], writes=[L[1]])
    rl, r_rl = b.ef.next()
    P.add("dve", lambda e: e.reciprocal(out=rl[:, 0:qn], in_=L[0]), reads=[L[1]], writes=[r_rl])
    ob, r_ob = b.obf.next()
    P.add("dve", lambda e: e.tensor_tensor(out=ob[:, 0:qn], in0=O[0], in1=rl[:, 0:qn], op=ALU.mult),
          reads=[O[1], r_rl], writes=[r_ob])
    P.add("sp", lambda e: e.dma_start(out=aoT[h * 128:(h + 1) * 128, q0:q0 + qn], in_=ob[:, 0:qn]),
          reads=[r_ob], writes=[r_ao_w[h]], dma=True)


def build_stage_mla_attn():
    b = Builder()
    P = b.P
    xin = b.inp("xT", [D, 2048])
    modv = b.inp("modv", [128, 96, 2])
    n2g = b.inp("n2g", [128, 16])
    QnT = b.inp("QnT", [16, 128, 2048], BF16)
    QrT = b.inp("QrT", [16, 64, 2048], BF16)
    KnT_all = b.inp("KnT_all", [16, 128, 8448], BF16)
    KrT_all = b.inp("KrT_all", [64, 8448], BF16)
    V_all = b.inp("V_all", [8448, 2048], BF16)
    w_o = b.inp("w_o", [D, D])
    w1 = b.inp("w1", [D, 8192])
    w2 = b.inp("w2", [8192, D])
    xo = b.out("xo", [D, 2048])
    xmid = b.scratch("xmid", [D, 2048])
    aoT = b.scratch("aoT", [D, 2048], BF16)
    b.setup_common(n_w=3)
    setup_attn_buffers(b)
    A2, B2, G2 = emit_modprep(b, modv, n2g, 3, 4, 5)
    G1 = (b.last_modv[0][:, 2 * 16:3 * 16, :], b.last_modv[1])
    krs = b.sb([64, 8448], BF16)
    r_kr = Res("kr")
    P.add("sp", lambda e: e.dma_start(out=krs[:, :], in_=KrT_all), writes=[r_kr], dma=True)
    qblocks = [(0, 512), (512, 512), (1024, 512), (1536, 512)]
    r_ao = [[Res() for _ in qblocks] for _ in range(KC)]
    Qrot = b.rot_sb(2, [128, 2, 2048], BF16)
    scale = 192 ** -0.5
    par = 0
    for h in range(16):
        Qh, r_q = Qrot.next()
        P.add("sp", lambda e: e.dma_start(out=Qh[:, 0, :], in_=QnT[h]), writes=[r_q], dma=True)
        P.add("sp", lambda e: e.dma_start(out=Qh[0:64, 1, :], in_=QrT[h]), writes=[r_q], dma=True)
        for qi, (q0, qn) in enumerate(qblocks):
            r_w = [r_ao[k][qi] for k in range(KC)]
            emit_mla_attn(b, h, q0, qn, KG, Qh, r_q, krs, r_kr, KnT_all, V_all, aoT, r_w, scale, par)
            par ^= 1
    emit_oproj_ffn(b, tiles_for(False), aoT, r_ao, w_o, xin, xmid, xo, w1, w2, A2, B2, G1, G2, qblocks)
    return b


_PROGS = {}


def _prog(name, fn):
    if name not in _PROGS:
        _PROGS[name] = fn().finish()
    return _PROGS[name]


def _run(nc, in_maps):
    res = run_bass_kernel_spmd(nc, in_maps, core_ids=list(range(8)))
    return res.results


def _vecT(v):
    return np.ascontiguousarray(np.asarray(v, np.float32).reshape(-1, 128).T)


def _make_prot(DR):
    hf = DR // 4
    p = np.zeros((DR, DR), np.float32)
    for m in range(DR):
        if (m // hf) % 2 == 0:
            p[m + hf, m] = -1.0
        else:
            p[m - hf, m] = 1.0
    return p


def _rope_tables(rot_dim):
    a = rot_dim // 2
    inv = np.power(np.float32(10000.0), -np.arange(0, a, 2, dtype=np.float32) / np.float32(a)).astype(np.float32)
    rows = np.repeat(np.arange(SEQ // 64, dtype=np.int32), 64)
    cols = np.tile(np.arange(64, dtype=np.int32), SEQ // 64)

    def ax(pos):
        ang = pos.astype(np.float32)[:, None] * inv[None, :]
        return np.concatenate([ang, ang], axis=-1)
    ang = np.concatenate([ax(rows), ax(cols)], axis=-1).astype(np.float32)
    return np.cos(ang).astype(np.float32), np.sin(ang).astype(np.float32)


def kernel(x, c, ctx, c_ctx, w_mod, b_mod, norm1_g, norm2_g, w_ff1, w_ff2,
           a_w_in, a_b_in, a_ln_g, a_w_s, a_b_s, a_w_out,
           b_w_qkv, b_lam_q1, b_lam_k1, b_lam_q2, b_lam_k2, b_subln_g, b_w_o,
           c_w_dqkv, c_q_norm_g, c_w_uq, c_kv_norm_g, c_w_ukv, c_w_o, final_g):
    f32 = np.float32
    x = np.asarray(x, f32)
    ctx = np.asarray(ctx, f32)
    w_mod = np.asarray(w_mod, f32)
    b_mod = np.asarray(b_mod, f32)
    cores = list(range(8))
    c_all = np.concatenate([np.asarray(c, f32), np.asarray(c_ctx, f32)[None]], 0)
    cT = np.ascontiguousarray(c_all.reshape(3, 16, 128).transpose(2, 1, 0))
    ins = []
    for i in cores:
        bm = b_mod[:, i * 1536:(i + 1) * 1536].reshape(4, 12, 128).transpose(2, 0, 1).reshape(128, 48)
        ins.append({"wm": np.ascontiguousarray(w_mod[:, :, i * 1536:(i + 1) * 1536]),
                    "bm": np.ascontiguousarray(bm), "cT": cT})
    rm = _run(_prog("mod", build_stage_mod), ins)
    mod_full = np.zeros((4, 3, 12288), f32)
    for i in cores:
        mo = rm[i]["modo"].reshape(128, 4, 12, 3)
        mod_full[:, :, i * 1536:(i + 1) * 1536] = mo.transpose(1, 3, 2, 0).reshape(4, 3, 1536)

    def modv_for(l, b):
        rows = [mod_full[l, row].reshape(6, 16, 128).transpose(2, 0, 1).reshape(128, 96) for row in (b, 2)]
        return np.ascontiguousarray(np.stack(rows, -1))

    def gmlp_consts(idx):
        return {"w_in": np.asarray(a_w_in[idx], f32), "b_u": _vecT(a_b_in[idx][:2048]),
                "b_v": np.ascontiguousarray(np.broadcast_to(np.asarray(a_b_in[idx][2048:], f32), (128, 2048))),
                "ln_g": _vecT(a_ln_g[idx]),
                "wsT": np.ascontiguousarray(np.asarray(a_w_s[idx], f32).transpose(2, 0, 1)),
                "b_s": np.ascontiguousarray(np.broadcast_to(np.asarray(a_b_s[idx], f32)[None], (128, 16, 128))),
                "w_out": np.asarray(a_w_out[idx], f32)}

    L = 0
    ins = []
    g0 = gmlp_consts(0)
    for r in cores:
        b, q = r // 4, r % 4
        xT = np.concatenate([x[b, q * 2048:(q + 1) * 2048].T, ctx[b].T], 1)
        m = {"xT": np.ascontiguousarray(xT), "modv": modv_for(L, b), "n1g": _vecT(norm1_g[L]), "n2g": _vecT(norm2_g[L]),
             "w1": np.asarray(w_ff1[L], f32), "w2": np.asarray(w_ff2[L], f32)}
        m.update(g0)
        ins.append(m)
    r0 = _run(_prog("gmlp_ctx", lambda: build_stage_gmlp_layer(True, False)), ins)
    xT_cur = [r0[r]["xo"] for r in cores]

    L = 1
    cos_b, sin_b = _rope_tables(128)
    prot128 = _make_prot(128)
    ins = []
    for r in cores:
        b, q = r // 4, r % 4
        ins.append({"xT": xT_cur[r], "modv": modv_for(L, b), "n1g": _vecT(norm1_g[L]), "w_qkv": np.asarray(b_w_qkv[0], f32),
                    "cosT": np.ascontiguousarray(cos_b[q * 2048:(q + 1) * 2048].T),
                    "sinT": np.ascontiguousarray(sin_b[q * 2048:(q + 1) * 2048].T), "prot": prot128})
    ra = _run(_prog("diff_qkv", build_stage_diff_qkv), ins)
    lam_init = 0.8 - 0.6 * math.exp(-0.3 * L)
    lamv = np.stack([np.asarray(v[0], f32) for v in (b_lam_q1, b_lam_k1, b_lam_q2, b_lam_k2)], 0)
    lamv = np.ascontiguousarray(np.broadcast_to(lamv[None], (128, 4, 128)))
    ins = []
    for r in cores:
        b, q = r // 4, r % 4
        cs = [b * 4 + i for i in range(4)]
        KT_all = np.concatenate([ra[cc]["KT"][:, :, :2048] for cc in cs] + [ra[cs[0]]["KT"][:, :, 2048:]], 2)
        V_all = np.concatenate([ra[cc]["V"][:2048] for cc in cs] + [ra[cs[0]]["V"][2048:]], 0)
        ins.append({"xT": xT_cur[r], "modv": modv_for(L, b), "n2g": _vecT(norm2_g[L]), "QT": ra[r]["QT"],
                    "KT_all": np.ascontiguousarray(KT_all), "V_all": np.ascontiguousarray(V_all), "lamv": lamv,
                    "subg": _vecT(b_subln_g[0]), "w_o": np.asarray(b_w_o[0], f32),
                    "w1": np.asarray(w_ff1[L], f32), "w2": np.asarray(w_ff2[L], f32)})
    rb = _run(_prog("diff_attn", lambda: build_stage_diff_attn(lam_init)), ins)
    xT_cur = [rb[r]["xo"] for r in cores]

    L = 2
    cos_c, sin_c = _rope_tables(64)
    prot64 = _make_prot(64)
    qg = np.zeros(512, f32)
    qg[:448] = np.asarray(c_q_norm_g[0], f32)
    ins = []
    for r in cores:
        b, q = r // 4, r % 4
        ins.append({"xT": xT_cur[r], "modv": modv_for(L, b), "n1g": _vecT(norm1_g[L]),
                    "w_dqkv": np.asarray(c_w_dqkv[0], f32), "qg": _vecT(qg), "kvg": _vecT(c_kv_norm_g[0]),
                    "w_uq": np.asarray(c_w_uq[0], f32), "w_ukv": np.asarray(c_w_ukv[0], f32),
                    "cosT": np.ascontiguousarray(cos_c[q * 2048:(q + 1) * 2048].T),
                    "sinT": np.ascontiguousarray(sin_c[q * 2048:(q + 1) * 2048].T), "prot": prot64})
    ra = _run(_prog("mla_qkv", build_stage_mla_qkv), ins)
    ins = []
    for r in cores:
        b, q = r // 4, r % 4
        cs = [b * 4 + i for i in range(4)]
        Kn = np.concatenate([ra[cc]["KnT"][:, :, :2048] for cc in cs] + [ra[cs[0]]["KnT"][:, :, 2048:]], 2)
        Kr = np.concatenate([ra[cc]["KrT"][:, :2048] for cc in cs] + [ra[cs[0]]["KrT"][:, 2048:]], 1)
        Va = np.concatenate([ra[cc]["V"][:2048] for cc in cs] + [ra[cs[0]]["V"][2048:]], 0)
        ins.append({"xT": np.ascontiguousarray(xT_cur[r][:, :2048]), "modv": modv_for(L, b), "n2g": _vecT(norm2_g[L]),
                    "QnT": ra[r]["QnT"], "QrT": ra[r]["QrT"], "KnT_all": np.ascontiguousarray(Kn),
                    "KrT_all": np.ascontiguousarray(Kr), "V_all": np.ascontiguousarray(Va),
                    "w_o": np.asarray(c_w_o[0], f32), "w1": np.asarray(w_ff1[L], f32), "w2": np.asarray(w_ff2[L], f32)})
    rb = _run(_prog("mla_attn", build_stage_mla_attn), ins)
    xT_cur = [rb[r]["xo"] for r in cores]

    L = 3
    g1 = gmlp_consts(1)
    ins = []
    for r in cores:
        b, q = r // 4, r % 4
        m = {"xT": xT_cur[r], "modv": modv_for(L, b), "n1g": _vecT(norm1_g[L]), "n2g": _vecT(norm2_g[L]),
             "w1": np.asarray(w_ff1[L], f32), "w2": np.asarray(w_ff2[L], f32), "fg": _vecT(final_g)}
        m.update(g1)
        ins.append(m)
    r3 = _run(_prog("gmlp_final", lambda: build_stage_gmlp_layer(False, True)), ins)
    out = np.empty((2, SEQ, D), f32)
    for r in cores:
        b, q = r // 4, r % 4
        out[b, q * 2048:(q + 1) * 2048, :] = r3[r]["xo"].T
    return out
```
